# Optimizing a Trainium2 kernel written in Bass

```python
import jax, jax.numpy as jnp
from jax import lax
import numpy as np

D_MODEL = 1024
BATCH = 32
SEQ = 2048
DEPTH = 2

CHUNK = 64
D_MIX = D_MODEL
D_CONV = D_MIX // 4
CONV_WIDTH = 3
D_FOX = D_MIX // 2
FOX_HEAD_DIM = 64
N_FOX_HEADS = D_FOX // FOX_HEAD_DIM
FOX_BLOCK = 128
D_SGU = D_MIX - D_CONV - D_FOX
N_SGU_GROUPS = 4
SGU_GROUP_DIM = D_SGU // N_SGU_GROUPS
SGU_CHUNK = 128
D_FF = 2816
ALPHA = (2 * DEPTH) ** 0.25
BETA = (8 * DEPTH) ** -0.25
LN_EPS = 1e-5
SPLIT_SIZES = (D_CONV, D_CONV, D_CONV, D_FOX, D_FOX, D_FOX, N_FOX_HEADS, D_SGU, D_SGU)
D_IN = 3 * D_CONV + 3 * D_FOX + N_FOX_HEADS + 2 * D_SGU

kernel_name = "hybrid_conv_fox_sgu_deepnorm_macaron"


def layer_norm(x, g, b):
    xf = x.astype(jnp.float32)
    mu = jnp.mean(xf, axis=-1, keepdims=True)
    xc = xf - mu
    var = jnp.mean(xc * xc, axis=-1, keepdims=True)
    y = xc * lax.rsqrt(var + LN_EPS) * g.astype(jnp.float32) + b.astype(jnp.float32)
    return y.astype(x.dtype)


def swiglu(x, w_up, w_down):
    gate, up = jnp.split(x @ w_up, 2, axis=-1)
    return (jax.nn.silu(gate) * up) @ w_down


def short_conv(gate_b, gate_c, h, w_conv):
    z = gate_c * h
    y = lax.conv_general_dilated(
        z, w_conv[:, None, :].astype(z.dtype),
        window_strides=(1,), padding=[(CONV_WIDTH - 1, 0)],
        dimension_numbers=("NWC", "WIO", "NWC"),
        feature_group_count=D_CONV)
    return gate_b * y


def forgetting_attention(q, k, v, f_logit):
    seq = q.shape[1]
    scale = FOX_HEAD_DIM ** -0.5
    log_f = jax.nn.log_sigmoid(f_logit.astype(jnp.float32))
    cum = jnp.transpose(jnp.cumsum(log_f, axis=1), (0, 2, 1))
    qf = q.astype(jnp.float32) * scale
    kf = k.astype(jnp.float32)
    vf = v.astype(jnp.float32)
    outs = []
    for i in range(seq // FOX_BLOCK):
        lo, hi = i * FOX_BLOCK, (i + 1) * FOX_BLOCK
        s = jnp.einsum('bqhd,bkhd->bhqk', qf[:, lo:hi], kf[:, :hi])
        s = s + cum[:, :, lo:hi, None] - cum[:, :, None, :hi]
        mask = jnp.arange(hi)[None, :] <= jnp.arange(lo, hi)[:, None]
        p = jax.nn.softmax(jnp.where(mask, s, -jnp.inf), axis=-1)
        outs.append(jnp.einsum('bhqk,bkhd->bqhd', p, vf[:, :hi]))
    return jnp.concatenate(outs, axis=1).astype(v.dtype)


def spatial_gating(u, v, ln_g, ln_b, w_s, b_s):
    bsz, seq, _ = v.shape
    u = jax.nn.gelu(u)
    v = layer_norm(jax.nn.gelu(v), ln_g, ln_b)
    vg = v.reshape(bsz, seq // SGU_CHUNK, SGU_CHUNK, N_SGU_GROUPS, SGU_GROUP_DIM)
    causal = jnp.tril(jnp.ones((SGU_CHUNK, SGU_CHUNK), dtype=w_s.dtype))
    mixed = jnp.einsum('gts,bnsgc->bntgc', w_s * causal, vg) + jnp.transpose(b_s)[:, :, None]
    return u * mixed.reshape(bsz, seq, D_SGU)


def hybrid_mixer(x, w_in, b_f, w_conv, sgu_ln_g, sgu_ln_b, w_s, b_s, w_out):
    bsz, seq, _ = x.shape
    offsets = []
    acc = 0
    for n in SPLIT_SIZES[:-1]:
        acc += n
        offsets.append(acc)
    proj = x @ w_in
    cb, cc, ch, q, k, v, f_logit, su, sv = jnp.split(proj, offsets, axis=-1)
    y_a = short_conv(cb, cc, ch, w_conv)
    heads = (bsz, seq, N_FOX_HEADS, FOX_HEAD_DIM)
    y_b = forgetting_attention(q.reshape(heads), k.reshape(heads), v.reshape(heads),
                               f_logit + b_f).reshape(bsz, seq, D_FOX)
    y_c = spatial_gating(su, sv, sgu_ln_g, sgu_ln_b, w_s, b_s)
    return jnp.concatenate([y_a, y_b, y_c], axis=-1) @ w_out


def setup_inputs(seed: int = 0) -> dict:
    key = jax.random.key(seed)
    ks = jax.random.split(key, 20)
    nrm = lambda k, shape, s: jax.random.normal(k, shape, jnp.float32) * s
    L, D = DEPTH, D_MODEL
    return {
        "x": nrm(ks[0], (BATCH, SEQ, D), 1.0),
        "ln1_g": 1.0 + nrm(ks[1], (L, D), 0.05),
        "ln1_b": nrm(ks[2], (L, D), 0.02),
        "ffn1_w_up": nrm(ks[3], (L, D, 2 * D_FF), D ** -0.5),
        "ffn1_w_down": nrm(ks[4], (L, D_FF, D), BETA * D_FF ** -0.5),
        "mix_w_in": nrm(ks[5], (L, D, D_IN), D ** -0.5),
        "fox_b_f": 4.0 + nrm(ks[6], (L, N_FOX_HEADS), 0.5),
        "conv_w": nrm(ks[7], (L, CONV_WIDTH, D_CONV), CONV_WIDTH ** -0.5),
        "sgu_ln_g": 1.0 + nrm(ks[8], (L, D_SGU), 0.05),
        "sgu_ln_b": nrm(ks[9], (L, D_SGU), 0.02),
        "sgu_w_s": nrm(ks[10], (L, N_SGU_GROUPS, SGU_CHUNK, SGU_CHUNK), SGU_CHUNK ** -0.5),
        "sgu_b_s": 1.0 + nrm(ks[11], (L, N_SGU_GROUPS, SGU_CHUNK), 0.02),
        "mix_w_out": nrm(ks[12], (L, D_MIX, D), BETA * D_MIX ** -0.5),
        "ln2_g": 1.0 + nrm(ks[13], (L, D), 0.05),
        "ln2_b": nrm(ks[14], (L, D), 0.02),
        "ffn2_w_up": nrm(ks[15], (L, D, 2 * D_FF), D ** -0.5),
        "ffn2_w_down": nrm(ks[16], (L, D_FF, D), BETA * D_FF ** -0.5),
        "ln3_g": 1.0 + nrm(ks[17], (L, D), 0.05),
        "ln3_b": nrm(ks[18], (L, D), 0.02),
    }


def reference(x, ln1_g, ln1_b, ffn1_w_up, ffn1_w_down, mix_w_in, fox_b_f, conv_w,
              sgu_ln_g, sgu_ln_b, sgu_w_s, sgu_b_s, mix_w_out, ln2_g, ln2_b,
              ffn2_w_up, ffn2_w_down, ln3_g, ln3_b):
    for l in range(DEPTH):
        x = layer_norm(ALPHA * x + 0.5 * swiglu(x, ffn1_w_up[l], ffn1_w_down[l]), ln1_g[l], ln1_b[l])
        mix = hybrid_mixer(x, mix_w_in[l], fox_b_f[l], conv_w[l], sgu_ln_g[l], sgu_ln_b[l],
                           sgu_w_s[l], sgu_b_s[l], mix_w_out[l])
        x = layer_norm(ALPHA * x + mix, ln2_g[l], ln2_b[l])
        x = layer_norm(ALPHA * x + 0.5 * swiglu(x, ffn2_w_up[l], ffn2_w_down[l]), ln3_g[l], ln3_b[l])
    return x
```

```python
import numpy as np
import os
DBG = os.environ.get('KDBG', '')
KSTEP = int(os.environ.get('KSTEP', '99'))
import concourse.bass as bass
import concourse.mybir as mybir
from concourse.bass_utils import run_bass_kernel_spmd

F32 = mybir.dt.float32
BF16 = mybir.dt.bfloat16
AF = mybir.ActivationFunctionType
ALU = mybir.AluOpType

D = 1024
KC = 8
DFF = 2816
NJ = 22
NJH = 11
DEPTH = 2
ALPHA = (2 * DEPTH) ** 0.25
LN_EPS = 1e-5
NH = 8
SLOT = 2112

ENG = ("pe", "act", "dve", "pool", "sp")


class Tile:
    __slots__ = ("ap", "w", "r", "name", "excl")

    def __init__(self, ap, name="", excl=False):
        self.excl = excl
        self.ap = ap
        self.w = None
        self.r = {}
        self.name = name


class MultiTile:
    __slots__ = ("ap", "subs")

    def __init__(self, ap, subs):
        self.ap = ap
        self.subs = subs


def _flat(ts_):
    out = []
    for t in ts_:
        if isinstance(t, MultiTile):
            out.extend(t.subs)
        else:
            out.append(t)
    return out


class Prog:
    def __init__(self):
        self.streams = {e: [] for e in ENG}
        self.cnt = {}
        self.waited = {e: {} for e in ENG}

    def _deps(self, reads, writes, eng=None):
        d = {}
        reads = _flat(reads)
        writes = _flat(writes)

        def add(k, v):
            if d.get(k, 0) < v:
                d[k] = v

        for t in reads:
            if t.w is not None:
                add(*t.w)
            if t.excl:
                for k, v in t.r.items():
                    if k != eng:
                        add(k, v)
        for t in writes:
            if t.w is not None:
                add(*t.w)
            for k, v in t.r.items():
                add(k, v)
        return d

    def _emit_waits(self, eng, d):
        st = self.streams[eng]
        wd = self.waited[eng]
        for k, v in d.items():
            if k == "pe" and eng == "pe":
                continue
            if wd.get(k, 0) < v:
                wd[k] = v
                st.append(("wait", k, v))

    def _commit(self, key, reads, writes):
        reads = _flat(reads)
        writes = _flat(writes)
        self.cnt[key] = self.cnt.get(key, 0) + 1
        v = self.cnt[key]
        for t in writes:
            t.w = (key, v)
            t.r = {}
        for t in reads:
            if t.r.get(key, 0) < v:
                t.r[key] = v
        return (key, v)

    def op(self, eng, fn, reads=(), writes=()):
        self._emit_waits(eng, self._deps(reads, writes, eng))
        self.streams[eng].append(("op", fn, eng, 1))
        return self._commit(eng, reads, writes)

    def group(self, eng, fns, reads=(), writes=()):
        self._emit_waits(eng, self._deps(reads, writes, eng))
        for f in fns[:-1]:
            self.streams[eng].append(("op", f, None, 0))
        self.streams[eng].append(("op", fns[-1], eng, 1))
        return self._commit(eng, reads, writes)

    def dma(self, eng, fn, semkey, reads=(), writes=()):
        self._emit_waits(eng, self._deps(reads, writes, eng))
        self.streams[eng].append(("op", fn, semkey, 16))
        self.cnt[semkey] = self.cnt.get(semkey, 0) + 15
        return self._commit(semkey, reads, writes)

    def final_wait(self, eng, keys):
        d = {k: self.cnt[k] for k in keys if self.cnt.get(k, 0) > 0}
        self._emit_waits(eng, d)


def build_nc(NSEQ=4, NTB=4, LAYERS=(0, 1), NL=2, STAGE="full"):
    S = NTB * 512
    NCH = S // 128
    NQB = S // 256
    nc = bass.Bass("TRN2", target_bir_lowering=False)
    P = Prog()

    def din(name, shape, dt=F32):
        return nc.dram_tensor(name, list(shape), dt, kind="ExternalInput").ap()

    xT = din("xT", [NSEQ, D, S])
    outT = nc.dram_tensor("outT", [NSEQ, D, S], F32, kind="ExternalOutput").ap()
    w_up = din("w_up", [NL, 2, NJ, 128, 2048])
    w_dn = din("w_dn", [NL, 2, 2, KC, 128, NJH * 128])
    w_in = din("w_in", [NL, 8, 128, 2048])
    w_fsv = din("w_fsv", [NL, 128, SLOT])
    w_v = din("w_v", [NL, 2, 128, 2048])
    w_o = din("w_o", [NL, 4, 128, 2048])
    c_lnp = din("c_lnp", [128, NL * 6 * KC])
    c_convw = din("c_convw", [128, NL * 3 * 2])
    c_sgb = din("c_sgb", [128, NL * 2 * 256])
    c_bf = din("c_bf", [128, NL * NH])
    c_wst = din("c_wst", [128, NL * 4 * 128])
    c_bsb = din("c_bsb", [128, NL * 2 * 128])
    c_tri = din("c_tri", [128, 128])

    from contextlib import ExitStack
    es = ExitStack()

    def sb(name, shape, dt):
        return es.enter_context(nc.sbuf_tensor(name, list(shape), dt))

    with es:
        x32_t = sb("x32", [128, KC, S], F32)
        xbf_t = sb("xbf", [128, KC, S], BF16)
        hb_t = sb("hb", [128, 12, S], BF16)
        wr_t = sb("wring", [128, 4, SLOT], BF16)
        sg_t = sb("sgr", [128, 3, 512], BF16)
        tmp_t = sb("tmp", [128, 7, 512], F32)
        z_t = sb("zc", [128, 2, 516], F32)
        er_t = sb("ering", [128, 4, 512], BF16)
        rec_t = sb("rec", [128, 1, 512], F32)
        vsg_t = sb("vsg", [128, 2, 256], BF16)
        lnp_t = sb("lnp", [128, NL * 6 * KC], F32)
        convw_t = sb("convw", [128, NL * 3 * 2], F32)
        sgb_t = sb("sgb", [128, NL * 2 * 256], F32)
        bf_t = sb("bfb", [128, NL * NH], F32)
        wst_t = sb("wst", [128, NL * 4 * 128], BF16)
        bsb_t = sb("bsb", [128, NL * 2 * 128], F32)
        tri_t = sb("tri", [128, 128], F32)
        trib_t = sb("trib", [128, 128], BF16)
        ones32_t = sb("ones32", [128, 128], F32)
        onesb_t = sb("onesb", [128, 128], BF16)
        zf_t = sb("zf", [128, NCH * NH], F32)
        lf_t = sb("lf", [128, 4, NCH * NH], F32)
        offs_t = sb("offs", [128, (NCH + 1) * NH], F32)
        cref_t = sb("cref", [128, 2, NTB * NH], F32)
        cum_t = sb("cum", [128, NCH * NH], F32)
        bias_t = sb("bias", [128, 2 * NTB * (NTB + 1) * NH], F32)
        st6_t = sb("st6", [128, 4, 8, 4], F32)
        ps_t = [es.enter_context(nc.psum_tensor(f"ps{i}", [128, 512], F32)) for i in range(8)]

        sems = {}

        def sem(key):
            if key not in sems:
                sems[key] = es.enter_context(nc.semaphore("s_" + key))
            return sems[key]

        x32 = [[Tile(x32_t[:, c, tb * 512:(tb + 1) * 512]) for tb in range(NTB)] for c in range(KC)]
        xbfn = [[Tile(xbf_t[:, c, n * 128:(n + 1) * 128]) for n in range(NCH)] for c in range(KC)]
        xbf = [[MultiTile(xbf_t[:, c, tb * 512:(tb + 1) * 512], xbfn[c][4 * tb:4 * tb + 4]) for tb in range(NTB)] for c in range(KC)]
        hb = [[Tile(hb_t[:, r, i * 256:(i + 1) * 256]) for i in range(NQB)] for r in range(12)]

        def hb_ap(r, tb):
            return hb_t[:, r, tb * 512:(tb + 1) * 512]

        def hb_tl(r, tb):
            return [hb[r][2 * tb], hb[r][2 * tb + 1]]

        wslot = [Tile(wr_t[:, i, :]) for i in range(4)]
        sgr = [Tile(sg_t[:, i, :]) for i in range(3)]
        gvt = [Tile(tmp_t[:, i // 2, (i % 2) * 256:(i % 2 + 1) * 256]) for i in range(8)]
        tmp = [MultiTile(tmp_t[:, i, :], gvt[2 * i:2 * i + 2]) for i in range(4)] + [Tile(tmp_t[:, i, :]) for i in range(4, 7)]
        zc = [Tile(z_t[:, i, :]) for i in range(2)]
        ering = [Tile(er_t[:, i, :]) for i in range(4)]
        recr = [Tile(rec_t[:, i, :]) for i in range(1)]
        vsgr = [Tile(vsg_t[:, i, :]) for i in range(2)]
        st6 = [Tile(st6_t[:, i, :, :]) for i in range(4)]
        ps = [Tile(ps_t[i][:, :], excl=True) for i in range(8)]
        lnp = Tile(lnp_t[:, :]); convw = Tile(convw_t[:, :]); sgb = Tile(sgb_t[:, :]); bfb = Tile(bf_t[:, :])
        wst = Tile(wst_t[:, :]); bsb = Tile(bsb_t[:, :])
        tri = Tile(tri_t[:, :]); trib = Tile(trib_t[:, :]); ones32 = Tile(ones32_t[:, :]); onesb = Tile(onesb_t[:, :])
        zf = Tile(zf_t[:, :]); lf = [Tile(lf_t[:, i, :]) for i in range(4)]
        offs = Tile(offs_t[:, :]); cum = Tile(cum_t[:, :])
        biasT = {(qp, kc): Tile(bias_t[:, (2 * qp * (qp + 1) + kc) * NH:(2 * qp * (qp + 1) + kc + 1) * NH]) for qp in range(NTB) for kc in range(4 * qp + 4)}
        cref = Tile(cref_t[:, :, :])

        class Ring:
            def __init__(self, tiles):
                self.tiles = tiles
                self.i = 0

            def next(self):
                t = self.tiles[self.i % len(self.tiles)]
                self.i += 1
                return t

        psA = Ring(ps[0:4])
        psB = Ring(ps[4:6])
        psC = Ring(ps[6:8])
        psO = Ring(ps[4:8])
        sgR = Ring(sgr); tmpR = Ring(tmp[4:7]); eR = Ring(ering); recR = Ring(recr); vsgR = Ring(vsgr); st6R = Ring(st6)
        wR = Ring(wslot)

        def mm(out, lhsT, rhs, start, stop):
            return lambda e: e.matmul(out, lhsT, rhs, start=start, stop=stop)

        def act(out, in_, func, reads, writes, bias=None, scale=None):
            kw = {}
            if bias is not None:
                kw["bias"] = bias
            if scale is not None:
                kw["scale"] = scale
            return P.op("act", lambda e: e.activation(out=out, in_=in_, func=func, **kw), reads, writes)

        def tt(out, in0, in1, op, reads, writes, eng="dve"):
            return P.op(eng, lambda e: e.tensor_tensor(out=out, in0=in0, in1=in1, op=op), reads, writes)

        def ts(out, in0, s1, s2, op0, op1, reads, writes):
            if op1 is None:
                return P.op("dve", lambda e: e.tensor_scalar(out=out, in0=in0, scalar1=s1, scalar2=None, op0=op0), reads, writes)
            return P.op("dve", lambda e: e.tensor_scalar(out=out, in0=in0, scalar1=s1, scalar2=s2, op0=op0, op1=op1), reads, writes)

        def stt(out, in0, scalar, in1, op0, op1, reads, writes):
            return P.op("dve", lambda e: e.scalar_tensor_tensor(out=out, in0=in0, scalar=scalar, in1=in1, op0=op0, op1=op1), reads, writes)

        def load_w(src_ap, F):
            t = wR.next()
            idx = wslot.index(t)
            dst = wr_t[:, idx, 0:F]
            if F % 1024 == 0:
                piece = 1024
            elif F == SLOT:
                piece = 704
            else:
                piece = 704
            for a in range(F // piece):
                dv = dst[:, a * piece:(a + 1) * piece]
                sv = src_ap[:, a * piece:(a + 1) * piece]
                P.dma("pool", lambda e, dv=dv, sv=sv: e.dma_start(out=dv, in_=sv), f"w{idx}", reads=(), writes=[t])
            return t, dst

        cnt_c = [0]

        def cload(tile_, dst_ap, src_ap):
            cnt_c[0] += 1
            P.dma("sp", lambda e: e.dma_start(out=dst_ap, in_=src_ap), f"c{cnt_c[0]}", reads=(), writes=[tile_])

        cload(lnp, lnp_t[:, :], c_lnp[:, :]); cload(convw, convw_t[:, :], c_convw[:, :])
        cload(sgb, sgb_t[:, :], c_sgb[:, :]); cload(bfb, bf_t[:, :], c_bf[:, :])
        wst32_ap = tmp_t[:, 0:2, :].rearrange('p a b -> p (a b)')
        for _i in range(2):
            cload(tmp[_i], tmp_t[:, _i, :], c_wst[:, _i * 512:(_i + 1) * 512])
        cload(bsb, bsb_t[:, :], c_bsb[:, :]); cload(tri, tri_t[:, :], c_tri[:, :])
        P.op("dve", lambda e: e.memset(ones32_t[:, :], 1.0), [], [ones32])
        P.op("dve", lambda e: e.memset(onesb_t[:, :], 1.0), [], [onesb])
        P.op("dve", lambda e: e.tensor_copy(out=trib_t[:, :], in_=tri_t[:, :]), [tri], [trib])
        for i in range(NL * 4):
            tt(wst_t[:, i * 128:(i + 1) * 128], wst32_ap[:, i * 128:(i + 1) * 128], tri_t[:, :], ALU.mult, [tmp[0], tmp[1], tri], [wst])

        def lnp_ap(l, i, c):
            o = (l * 6 + i) * KC + c
            return lnp_t[:, o:o + 1]

        def layer_norm(l, i, eps, after_tb=None):
            def stats_chunk(tb, c, S1, S2):
                act(xbf[c][tb].ap, x32[c][tb].ap, AF.Copy, [x32[c][tb]], [xbf[c][tb]])
                sq = sgR.next()
                act(sq.ap, x32[c][tb].ap, AF.Square, [x32[c][tb]], [sq])
                P.group("pe", [mm(S1.ap, onesb_t[:, :], xbf[c][tb].ap, c == 0, c == KC - 1)], [onesb, xbf[c][tb]], [S1])
                P.group("pe", [mm(S2.ap, onesb_t[:, :], sq.ap, c == 0, c == KC - 1)], [onesb, sq], [S2])

            def ab_chain(S1, S2):
                m2 = tmp[0]; var = tmp[1]; Asb = tmp[2]
                A = psB.next(); B = psB.next()
                act(m2.ap, S1.ap, AF.Square, [S1], [m2], scale=1.0 / D)
                stt(var.ap, S2.ap, 1.0 / D, m2.ap, ALU.mult, ALU.subtract, [S2, m2], [var])
                act(var.ap, var.ap, AF.Ln, [var], [var], bias=eps)
                act(Asb.ap, var.ap, AF.Exp, [var], [Asb], scale=-0.5)
                act(A.ap, Asb.ap, AF.Copy, [Asb], [A])
                stt(B.ap, S1.ap, -1.0 / D, Asb.ap, ALU.mult, ALU.mult, [S1, Asb], [B])
                return A, B

            def norm_group(tb, cs_, A, B):
                t1s = {c: tmpR.next() for c in cs_}
                for c in cs_:
                    tt(t1s[c].ap, x32[c][tb].ap, A.ap, ALU.mult, [x32[c][tb], A], [t1s[c]])
                for c in cs_:
                    tt(t1s[c].ap, t1s[c].ap, B.ap, ALU.add, [t1s[c], B], [t1s[c]])
                for c in cs_:
                    t1 = t1s[c]
                    ts(x32[c][tb].ap, t1.ap, lnp_ap(l, 2 * i, c), lnp_ap(l, 2 * i + 1, c), ALU.mult, ALU.add, [t1, lnp], [x32[c][tb]])
                    act(xbf[c][tb].ap, t1.ap, AF.Identity, [t1, lnp], [xbf[c][tb]], bias=lnp_ap(l, 2 * i + 1, c), scale=lnp_ap(l, 2 * i, c))

            Scur = (psC.next(), psC.next())
            for c in range(KC):
                stats_chunk(0, c, *Scur)
            AB = ab_chain(*Scur)
            for tb in range(NTB):
                nxt_ = tb + 1 < NTB
                if nxt_:
                    Scur = (psC.next(), psC.next())
                for c0 in range(0, KC, 3):
                    cs_ = list(range(c0, min(c0 + 3, KC)))
                    if nxt_:
                        for c in cs_:
                            stats_chunk(tb + 1, c, *Scur)
                    norm_group(tb, cs_, *AB)
                if nxt_:
                    AB = ab_chain(*Scur)
                if after_tb is not None:
                    after_tb(tb)

        def ffn_make(l, f):
            cres = 0.5 / ALPHA
            groups = [[0, 1], [2, 3, 4], [5, 6, 7], [8, 9, 10]]
            state = {}

            def up_tb(g, wt, tb):
                for jj in g:
                    wtile, wap = wt[jj]
                    w3 = wap.rearrange("p (k c) -> p k c", c=256)
                    G = psA.next(); U = psA.next()
                    rd = [wtile] + [xbf[kc][tb] for kc in range(KC)]
                    P.group("pe", [mm(G.ap, w3[:, kc, 0:128], xbf[kc][tb].ap, kc == 0, kc == KC - 1) for kc in range(KC)], rd, [G])
                    P.group("pe", [mm(U.ap, w3[:, kc, 128:256], xbf[kc][tb].ap, kc == 0, kc == KC - 1) for kc in range(KC)], rd, [U])
                    sg = sgR.next()
                    act(sg.ap, G.ap, AF.Silu, [G], [sg])
                    tt(hb_ap(jj, tb), sg.ap, U.ap, ALU.mult, [sg, U], hb_tl(jj, tb))

            def head(tb):
                if "wt" not in state:
                    state["wt"] = {jj: load_w(w_up[l, f, jj], 2048) for jj in groups[0]}
                up_tb(groups[0], state["wt"], tb)

            def rest():
                for hf in range(2):
                    for gi, g in enumerate(groups):
                        if hf == 0 and gi == 0:
                            continue
                        wt = {jj: load_w(w_up[l, f, hf * NJH + jj], 2048) for jj in g}
                        for tb in range(NTB):
                            up_tb(g, wt, tb)
                    for m in range(KC):
                        wtile, wap = load_w(w_dn[l, f, hf, m], NJH * 128)
                        w3 = wap.rearrange("p (j c) -> p j c", c=128)
                        for tb in range(NTB):
                            Y = psB.next()
                            rd = [wtile]
                            for jj in range(NJH):
                                rd += hb_tl(jj, tb)
                            P.group("pe", [mm(Y.ap, w3[:, jj, :], hb_ap(jj, tb), jj == 0, jj == NJH - 1) for jj in range(NJH)], rd, [Y])
                            stt(x32[m][tb].ap, Y.ap, cres, x32[m][tb].ap, ALU.mult, ALU.add, [Y, x32[m][tb]], [x32[m][tb]])

            return head, rest

        ROW_Q, ROW_K, ROW_U, ROW_YA = 0, 4, 8, 10

        def proj_fm(wtile, w3, c2, tb):
            Pp = psA.next()
            rd = [wtile] + [xbf[kc][tb] for kc in range(KC)]
            P.group("pe", [mm(Pp.ap, w3[:, kc, c2 * 128:(c2 + 1) * 128], xbf[kc][tb].ap, kc == 0, kc == KC - 1) for kc in range(KC)], rd, [Pp])
            return Pp

        def mixer_make(l):
            state = {}

            def conv_tb(tb):
                if "W" not in state:
                    WA = load_w(w_in[l, 0], 2048); WB = load_w(w_in[l, 1], 2048); WC = load_w(w_in[l, 2], 2048)
                    state["W"] = (WA, WB, WC)
                    for c in range(2):
                        P.op("dve", lambda e, c=c: e.memset(z_t[:, c, 0:2], 0.0), [], [zc[c]])
                WA, WB, WC = state["W"]
                WA3 = WA[1].rearrange("p (k c) -> p k c", c=256)
                WB3 = WB[1].rearrange("p (k c) -> p k c", c=256)
                WC3 = WC[1].rearrange("p (k c) -> p k c", c=256)
                for c in range(2):
                    def cw(i):
                        o = (l * 3 + i) * 2 + c
                        return convw_t[:, o:o + 1]
                    P1 = proj_fm(WA[0], WA3, c, tb)
                    cct = tmpR.next()
                    act(cct.ap, P1.ap, AF.Copy, [P1], [cct])
                    P2 = proj_fm(WB[0], WB3, c, tb)
                    tt(z_t[:, c, 2:514], P2.ap, cct.ap, ALU.mult, [P2, cct], [zc[c]])
                    yv = tmpR.next()
                    ts(yv.ap, z_t[:, c, 2:514], cw(2), None, ALU.mult, None, [zc[c], convw], [yv])
                    stt(yv.ap, z_t[:, c, 1:513], cw(1), yv.ap, ALU.mult, ALU.add, [zc[c], convw, yv], [yv])
                    stt(yv.ap, z_t[:, c, 0:512], cw(0), yv.ap, ALU.mult, ALU.add, [zc[c], convw, yv], [yv])
                    P3 = proj_fm(WC[0], WC3, c, tb)
                    tt(hb_ap(ROW_YA + c, tb), P3.ap, yv.ap, ALU.mult, [P3, yv], hb_tl(ROW_YA + c, tb))
                    P.op("dve", lambda e, c=c: e.tensor_copy(out=z_t[:, c, 0:2], in_=z_t[:, c, 512:514]), [zc[c]], [zc[c]])

            def rest():
                mixer_rest(l)

            return conv_tb, rest

        def mixer_rest(l):
            if STAGE == 'm1a':
                return
            for ti, row0, kind in ((3, ROW_Q, "c"), (4, ROW_Q + 2, "c"), (5, ROW_K, "c"), (6, ROW_K + 2, "c"), (7, ROW_U, "g")):
                W = load_w(w_in[l, ti], 2048)
                W3 = W[1].rearrange("p (k c) -> p k c", c=256)
                for tb in range(NTB):
                    for c2 in range(2):
                        Pp = proj_fm(W[0], W3, c2, tb)
                        fn = AF.Copy if kind == "c" else AF.Gelu_apprx_tanh
                        act(hb_ap(row0 + c2, tb), Pp.ap, fn, [Pp], hb_tl(row0 + c2, tb))
            if STAGE == 'm1b':
                return
            WF = load_w(w_fsv[l], SLOT); WV0 = load_w(w_v[l, 0], 2048); WV1 = load_w(w_v[l, 1], 2048)
            WF3 = WF[1].rearrange("p (k c) -> p k c", c=264)
            WV3 = [WV0[1].rearrange("p (k c) -> p k c", c=512), WV1[1].rearrange("p (k c) -> p k c", c=512)]
            GN = 4
            grp_s6 = {}

            def phaseA(gi):
                s6g = st6R.next()
                grp_s6[gi] = s6g
                for j in range(GN):
                    n = gi * GN + j
                    tb = n // 4
                    cs = slice((n % 4) * 128, (n % 4) * 128 + 128)
                    xs = [xbfn[kc][n] for kc in range(KC)]
                    T1 = psA.next()
                    P.group("pe", [mm(T1.ap[:, 0:264], xbf[kc][tb].ap[:, cs], WF3[:, kc, :], kc == 0, kc == KC - 1) for kc in range(KC)], [WF[0]] + xs, [T1])
                    T2 = psA.next()
                    fns = [mm(T2.ap[:, 0:512], xbf[kc][tb].ap[:, cs], WV3[kc // 4][:, kc % 4, :], kc == 0, kc == KC - 1) for kc in range(KC)]
                    P.group("pe", fns, [WV0[0], WV1[0]] + xs, [T2])
                    tt(zf_t[:, n * NH:(n + 1) * NH], T1.ap[:, 256:264], bf_t[:, l * NH:(l + 1) * NH], ALU.add, [T1, bfb], [zf])
                    gv = gvt[(gi % 2) * GN + j]
                    P.op("act", lambda e, gv=gv, T1=T1, j=j: e.activation(out=gv.ap, in_=T1.ap[:, 0:256], func=AF.Gelu_apprx_tanh, accum_out=s6g.ap[:, 0, j:j + 1]), [T1], [gv, s6g])
                    P.op("act", lambda e, gv=gv, j=j: e.activation(out=z_t[:, 0, 0:256], in_=gv.ap, func=AF.Square, accum_out=s6g.ap[:, 1, j:j + 1]), [gv], [zc[0], s6g])
                    T2v = T2.ap.rearrange("p (h d) -> p h d", d=64)
                    c0 = n * 128
                    P.op("act", lambda e, T2v=T2v, c0=c0: e.activation(out=xbf_t[:, 0:8:2, c0:c0 + 64], in_=T2v[:, 0:8:2, :], func=AF.Copy), [T2] + xs, xs)
                    P.op("act", lambda e, T2v=T2v, c0=c0: e.activation(out=xbf_t[:, 1:8:2, c0 + 64:c0 + 128], in_=T2v[:, 1:8:2, :], func=AF.Copy), [T2] + xs, xs)
                    P.op("dve", lambda e, c0=c0: e.memset(xbf_t[:, 0:8:2, c0 + 64:c0 + 128], 1.0), xs, xs)
                    P.op("dve", lambda e, c0=c0: e.memset(xbf_t[:, 1:8:2, c0:c0 + 64], 1.0), xs, xs)

            def phaseB(gi):
                s6g = grp_s6[gi]
                f = lambda k: s6g.ap[:, k, :]
                ts(f(2), f(0), 1.0 / 256, None, ALU.mult, None, [s6g], [s6g])
                tt(f(3), f(2), f(2), ALU.mult, [s6g], [s6g])
                stt(f(4), f(1), 1.0 / 256, f(3), ALU.mult, ALU.subtract, [s6g], [s6g])
                act(f(4), f(4), AF.Ln, [s6g], [s6g], bias=LN_EPS)
                act(f(5), f(4), AF.Exp, [s6g], [s6g], scale=-0.5)
                stt(f(6), f(2), -1.0, f(5), ALU.mult, ALU.mult, [s6g], [s6g])

            def phaseC(gi):
                s6g = grp_s6.pop(gi)
                for j in range(GN):
                    n = gi * GN + j
                    gv = gvt[(gi % 2) * GN + j]
                    act(gv.ap, gv.ap, AF.Identity, [gv, s6g], [gv], bias=s6g.ap[:, 6, j:j + 1], scale=s6g.ap[:, 5, j:j + 1])
                    tt(gv.ap, gv.ap, sgb_t[:, (l * 2) * 256:(l * 2 + 1) * 256], ALU.mult, [gv, sgb], [gv])
                    vs = vsgR.next()
                    tt(vs.ap, gv.ap, sgb_t[:, (l * 2 + 1) * 256:(l * 2 + 2) * 256], ALU.add, [gv, sgb], [vs])
                    for cp in range(2):
                        Mx = psC.next()
                        fns = []
                        for gg in range(2):
                            g4 = 2 * cp + gg
                            o = (l * 4 + g4) * 128
                            fns.append(mm(Mx.ap[gg * 64:(gg + 1) * 64, 0:128], vs.ap[:, g4 * 64:(g4 + 1) * 64], wst_t[:, o:o + 128], True, True))
                        P.group("pe", fns, [vs, wst], [Mx])
                        t1 = tmpR.next()
                        ob = (l * 2 + cp) * 128
                        tt(t1.ap[:, 0:128], Mx.ap[:, 0:128], bsb_t[:, ob:ob + 128], ALU.add, [Mx, bsb], [t1])
                        utl = hb[ROW_U + cp][n // 2]
                        uap = hb_t[:, ROW_U + cp, n * 128:(n + 1) * 128]
                        tt(uap, t1.ap[:, 0:128], uap, ALU.mult, [t1, utl], [utl])

            def m2_chain():
                NF = NCH * NH
                mn, ab, ex, l1 = lf
                ts(mn.ap, zf_t[:, :], 0.0, None, ALU.min, None, [zf], [mn])
                stt(ab.ap, mn.ap, 2.0, zf_t[:, :], ALU.mult, ALU.subtract, [mn, zf], [ab])
                act(ex.ap, ab.ap, AF.Exp, [ab], [ex])
                act(l1.ap, ex.ap, AF.Ln, [ex], [l1], bias=1.0)
                tt(mn.ap, mn.ap, l1.ap, ALU.subtract, [mn, l1], [mn])
                CW = psB.next(); TOT = psB.next()
                P.group("pe", [mm(CW.ap[:, 0:NF], tri_t[:, :], mn.ap, True, True)], [tri, mn], [CW])
                P.group("pe", [mm(TOT.ap[:, 0:NF], ones32_t[:, :], mn.ap, True, True)], [ones32, mn], [TOT])
                P.op("dve", lambda e: e.memset(offs_t[:, 0:NH], 0.0), [], [offs])
                for n in range(1, NCH + 1):
                    tt(offs_t[:, n * NH:(n + 1) * NH], offs_t[:, (n - 1) * NH:n * NH], TOT.ap[:, (n - 1) * NH:n * NH], ALU.add, [offs, TOT], [offs])
                tt(cum_t[:, :], CW.ap[:, 0:NF], offs_t[:, 0:NF], ALU.add, [CW, offs], [cum])
                for qp in range(NTB):
                    o0 = 4 * qp * NH
                    o1 = (4 * qp + 4) * NH
                    dq = cref_t[:, 0, qp * NH:(qp + 1) * NH]
                    cr = cref_t[:, 1, qp * NH:(qp + 1) * NH]
                    tt(dq, offs_t[:, o0:o0 + NH], offs_t[:, o1:o1 + NH], ALU.subtract, [offs], [cref])
                    ts(dq, dq, -60.0, 0.0, ALU.add, ALU.max, [cref], [cref])
                    tt(cr, offs_t[:, o0:o0 + NH], dq, ALU.subtract, [offs, cref], [cref])
                for qp in range(NTB):
                    cr = cref_t[:, 1, qp * NH:(qp + 1) * NH]
                    for kc in range(4 * qp + 4):
                        o = (2 * qp * (qp + 1) + kc) * NH
                        tt(bias_t[:, o:o + NH], cr, cum_t[:, kc * NH:(kc + 1) * NH], ALU.subtract, [cref, cum], [biasT[(qp, kc)]])

            NG = NCH // GN
            for gi in range(NG + 1):
                if gi < NG:
                    phaseA(gi)
                if gi == NG - 1 and STAGE not in ('m1c',):
                    m2_chain()
                if gi >= 1:
                    phaseC(gi - 1)
                if gi < NG:
                    phaseB(gi)
            if STAGE == 'm1c':
                return
            if STAGE == 'm2':
                return
            NQP = NQB // 2
            steps = []
            for c in range(NH // 2):
                for qp in range(NQP):
                    for kc in range(4 * qp + 4):
                        steps.append((c, qp, kc))
            Ocur = {}

            def emit_score(c, qp, kc):
                qts = [hb[ROW_Q + c][2 * qp], hb[ROW_Q + c][2 * qp + 1]]
                kt = hb[ROW_K + c][kc // 2]
                st0 = max(0, kc - 4 * qp) * 128
                STs = [psA.next(), psA.next()]
                Es = [eR.next(), eR.next()]
                fns = []
                for hp in range(2):
                    ps_ = slice(hp * 64, hp * 64 + 64)
                    kap = hb_t[ps_, ROW_K + c, kc * 128:(kc + 1) * 128]
                    qap = hb_t[ps_, ROW_Q + c, qp * 512 + st0:(qp + 1) * 512]
                    fns.append(mm(STs[hp].ap[:, st0:512], kap, qap, True, True))
                P.group("pe", fns, [kt] + qts, STs)
                for hp in range(2):
                    h = 2 * c + hp
                    ST = STs[hp]; E = Es[hp]
                    o = (2 * qp * (qp + 1) + kc) * NH + h
                    act(E.ap[:, st0:512], ST.ap[:, st0:512], AF.Exp, [ST, biasT[(qp, kc)]], [E], bias=bias_t[:, o:o + 1], scale=0.125)
                    if kc >= 4 * qp:
                        tt(E.ap[:, st0:st0 + 128], E.ap[:, st0:st0 + 128], trib_t[:, :], ALU.mult, [E, trib], [E])
                return Es

            def emit_pv(c, qp, kc, Es):
                nk = 4 * qp + 4
                st0 = max(0, kc - 4 * qp) * 128
                if kc == 0:
                    Ocur[(c, qp)] = [psO.next(), psO.next()]
                Os = Ocur[(c, qp)]
                fns = []
                rd = []
                for hp in range(2):
                    h = 2 * c + hp
                    rd += [xbfn[h][kc], Es[hp]]
                    vap = xbf_t[:, h, kc * 128:(kc + 1) * 128]
                    fns.append(mm(Os[hp].ap[:, st0:512], vap, Es[hp].ap[:, st0:512], kc == 0, kc == nk - 1))
                P.group("pe", fns, rd, Os)
                if kc == nk - 1:
                    qts = [hb[ROW_Q + c][2 * qp], hb[ROW_Q + c][2 * qp + 1]]
                    rc = recR.next()
                    for hp in range(2):
                        O = Os[hp]
                        ps_ = slice(hp * 64, hp * 64 + 64)
                        dn = slice(64, 128) if hp == 0 else slice(0, 64)
                        P.op("dve", lambda e, rc=rc, O=O, dn=dn, ps_=ps_: e.reciprocal(out=rc.ap[ps_, :], in_=O.ap[dn, :]), [O], [rc])
                        yap = hb_t[ps_, ROW_Q + c, qp * 512:(qp + 1) * 512]
                        tt(yap, O.ap[ps_, :], rc.ap[ps_, :], ALU.mult, [O, rc], qts)

            pend = []
            for st_ in steps:
                if len(pend) == 2:
                    st0_, E0 = pend.pop(0)
                    emit_pv(*st0_, E0)
                Es_ = emit_score(*st_)
                pend.append((st_, Es_))
            for st0_, E0 in pend:
                emit_pv(*st0_, E0)
            if STAGE == 'm3':
                return
            yrows = [ROW_YA, ROW_YA + 1, ROW_Q, ROW_Q + 1, ROW_Q + 2, ROW_Q + 3, ROW_U, ROW_U + 1]
            for mp in range(4):
                W = load_w(w_o[l, mp], 2048)
                W3 = W[1].rearrange("p (k c) -> p k c", c=256)
                for tb in range(NTB):
                    for m2 in range(2):
                        m = mp * 2 + m2
                        Y = psB.next()
                        rd = [W[0]]
                        for r in yrows:
                            rd += hb_tl(r, tb)
                        P.group("pe", [mm(Y.ap, W3[:, kc, m2 * 128:(m2 + 1) * 128], hb_ap(yrows[kc], tb), kc == 0, kc == KC - 1) for kc in range(KC)], rd, [Y])
                        stt(x32[m][tb].ap, Y.ap, 1.0 / ALPHA, x32[m][tb].ap, ALU.mult, ALU.add, [Y, x32[m][tb]], [x32[m][tb]])

        eps_s = LN_EPS / (ALPHA * ALPHA)
        def load_blk(s, tb):
            for c in range(KC):
                P.dma("sp", lambda e, c=c: e.dma_start(out=x32[c][tb].ap, in_=xT[s, c * 128:(c + 1) * 128, tb * 512:(tb + 1) * 512]), f"xi{c}_{tb}", reads=(), writes=[x32[c][tb]])
            for c in range(KC):
                act(xbf[c][tb].ap, x32[c][tb].ap, AF.Copy, [x32[c][tb]], [xbf[c][tb]])

        def store_blk(s, tb):
            for c in range(KC):
                P.dma("sp", lambda e, c=c: e.dma_start(out=outT[s, c * 128:(c + 1) * 128, tb * 512:(tb + 1) * 512], in_=x32[c][tb].ap), f"xo{c}_{tb}", reads=[x32[c][tb]], writes=())

        pre = None
        for s in range(NSEQ):
            if pre is None:
                for tb in range(NTB):
                    load_blk(s, tb)
                nxt = None
            else:
                nxt = pre
                pre = None
            for li, l in enumerate(LAYERS):
                if nxt is None:
                    f1_head, f1_rest = ffn_make(l, 0)
                    for tb in range(NTB):
                        f1_head(tb)
                else:
                    f1_head, f1_rest = nxt
                f1_rest()
                if STAGE == "ffn1":
                    layer_norm(l, 0, eps_s)
                    nxt = None
                    continue
                conv_tb, mix_rest = mixer_make(l)
                layer_norm(l, 0, eps_s, after_tb=conv_tb)
                mix_rest()
                if STAGE.startswith("m") and STAGE != "mix":
                    nxt = None
                    continue
                if STAGE == "mix":
                    layer_norm(l, 1, eps_s)
                    nxt = None
                    continue
                f2_head, f2_rest = ffn_make(l, 1)
                layer_norm(l, 1, eps_s, after_tb=f2_head)
                f2_rest()
                if li + 1 < len(LAYERS):
                    nxt = ffn_make(LAYERS[li + 1], 0)
                    layer_norm(l, 2, eps_s, after_tb=nxt[0])
                else:
                    nxt = None
                    if s + 1 < NSEQ:
                        pre = ffn_make(LAYERS[0], 0)

                    def io_hook(tb, s=s, pre=pre):
                        store_blk(s, tb)
                        if pre is not None:
                            load_blk(s + 1, tb)
                            pre[0](tb)

                    layer_norm(l, 2, eps_s, after_tb=io_hook)
            if STAGE != "full":
                for tb in range(NTB):
                    store_blk(s, tb)
        P.final_wait("sp", [f"xo{c}_{tb}" for c in range(KC) for tb in range(NTB)])

        for k in list(P.cnt.keys()):
            sem(k)

        def replay(engname):
            def body(e):
                for it in P.streams[engname]:
                    if it[0] == "wait":
                        e.wait_ge(sems[it[1]], it[2])
                    else:
                        ins = it[1](e)
                        if it[3]:
                            ins.then_inc(sems[it[2]], it[3])
            return body

        with nc.Block() as block:
            block.tensor(replay("pe"))
            block.scalar(replay("act"))
            block.vector(replay("dve"))
            block.gpsimd(replay("pool"))
            block.sync(replay("sp"))
    return nc


def _prep_weights(inp, NL=2):
    f32 = np.float32
    def up_tiles(w):
        a = w.reshape(NL, KC, 128, 2, NJ, 128)
        a = a.transpose(0, 4, 2, 1, 3, 5)
        return np.ascontiguousarray(a.reshape(NL, NJ, 128, 2048))
    def dn_tiles(w):
        a = w.reshape(NL, 2, NJH, 128, KC, 128)
        a = a.transpose(0, 1, 4, 3, 2, 5)
        return np.ascontiguousarray(a.reshape(NL, 2, KC, 128, NJH * 128))
    def col_tile(w, cols):
        a = w[:, :, cols].reshape(NL, KC, 128, len(cols))
        a = a.transpose(0, 2, 1, 3)
        return np.ascontiguousarray(a.reshape(NL, 128, KC * len(cols)))
    w_up = np.stack([up_tiles(inp["ffn1_w_up"]), up_tiles(inp["ffn2_w_up"])], axis=1).astype(f32)
    w_dn = np.stack([dn_tiles(inp["ffn1_w_down"]), dn_tiles(inp["ffn2_w_down"])], axis=1).astype(f32)
    wi = inp["mix_w_in"]
    r = np.arange
    tiles = [r(256, 512), r(512, 768), r(0, 256), r(768, 1024), r(1024, 1280), r(1280, 1536), r(1536, 1792), r(2312, 2568)]
    w_in = np.stack([col_tile(wi, t) for t in tiles], axis=1).astype(f32)
    w_fsv = col_tile(wi, np.concatenate([r(2568, 2824), r(2304, 2312)])).astype(f32)
    wv = wi[:, :, 1792:2304].reshape(NL, 2, 4, 128, 512).transpose(0, 1, 3, 2, 4)
    w_v = np.ascontiguousarray(wv.reshape(NL, 2, 128, 2048)).astype(f32)
    wo = inp["mix_w_out"]
    w_o = np.stack([col_tile(wo, r(mp * 256, (mp + 1) * 256)) for mp in range(4)], axis=1).astype(f32)
    lnl = [inp["ln1_g"], inp["ln1_b"], inp["ln2_g"], inp["ln2_b"], inp["ln3_g"], inp["ln3_b"]]
    lnp = np.stack(lnl, axis=1).reshape(NL, 6, KC, 128).transpose(3, 0, 1, 2).reshape(128, NL * 6 * KC)
    convw = inp["conv_w"].reshape(NL, 3, 2, 128).transpose(3, 0, 1, 2).reshape(128, NL * 3 * 2)
    sgb = np.stack([inp["sgu_ln_g"], inp["sgu_ln_b"]], axis=1).reshape(1, NL * 2 * 256)
    sgb = np.broadcast_to(sgb, (128, NL * 2 * 256))
    bfb = np.broadcast_to(inp["fox_b_f"].reshape(1, NL * NH), (128, NL * NH))
    wst = inp["sgu_w_s"].transpose(3, 0, 1, 2).reshape(128, NL * 4 * 128)
    bs = inp["sgu_b_s"].reshape(NL, 2, 2, 1, 128)
    bsb = np.broadcast_to(bs, (NL, 2, 2, 64, 128)).transpose(2, 3, 0, 1, 4).reshape(128, NL * 2 * 128)
    tri = (np.arange(128)[None, :] >= np.arange(128)[:, None]).astype(f32)
    c = lambda a: np.ascontiguousarray(a, dtype=f32)
    return dict(w_up=w_up, w_dn=w_dn, w_in=w_in, w_fsv=w_fsv, w_v=w_v, w_o=w_o, c_lnp=c(lnp), c_convw=c(convw),
                c_sgb=c(sgb), c_bf=c(bfb), c_wst=c(wst), c_bsb=c(bsb), c_tri=c(tri))


def kernel(**inputs):
    inp = {k: np.asarray(v) for k, v in inputs.items()}
    x = inp["x"]
    B, S, _ = x.shape
    ncores = 8
    per = B // ncores
    wd = _prep_weights(inp)
    nc = build_nc(NSEQ=per, NTB=S // 512)
    in_maps = []
    for i in range(ncores):
        m = dict(wd)
        m["xT"] = np.ascontiguousarray(x[i * per:(i + 1) * per].transpose(0, 2, 1))
        in_maps.append(m)
    res = run_bass_kernel_spmd(nc, in_maps, core_ids=list(range(ncores)))
    outs = [np.asarray(r["outT"]).transpose(0, 2, 1) for r in res.results]
    return np.ascontiguousarray(np.concatenate(outs, axis=0)).astype(np.float32)
```

```python
import numpy as np
import os
DBG = os.environ.get('KDBG', '')
KSTEP = int(os.environ.get('KSTEP', '99'))
import concourse.bass as bass
import concourse.mybir as mybir
from concourse.bass_utils import run_bass_kernel_spmd

F32 = mybir.dt.float32
BF16 = mybir.dt.bfloat16
AF = mybir.ActivationFunctionType
ALU = mybir.AluOpType

D = 1024
KC = 8
DFF = 2816
NJ = 22
NJH = 11
DEPTH = 2
ALPHA = (2 * DEPTH) ** 0.25
LN_EPS = 1e-5
NH = 8
SLOT = 2112

ENG = ("pe", "act", "dve", "pool", "sp")


class Tile:
    __slots__ = ("ap", "w", "r", "name", "excl")

    def __init__(self, ap, name="", excl=False):
        self.excl = excl
        self.ap = ap
        self.w = None
        self.r = {}
        self.name = name


class MultiTile:
    __slots__ = ("ap", "subs")

    def __init__(self, ap, subs):
        self.ap = ap
        self.subs = subs


def _flat(ts_):
    out = []
    for t in ts_:
        if isinstance(t, MultiTile):
            out.extend(t.subs)
        else:
            out.append(t)
    return out


class Prog:
    def __init__(self):
        self.streams = {e: [] for e in ENG}
        self.cnt = {}
        self.waited = {e: {} for e in ENG}

    def _deps(self, reads, writes, eng=None):
        d = {}
        reads = _flat(reads)
        writes = _flat(writes)

        def add(k, v):
            if d.get(k, 0) < v:
                d[k] = v

        for t in reads:
            if t.w is not None:
                add(*t.w)
            if t.excl:
                for k, v in t.r.items():
                    if k != eng:
                        add(k, v)
        for t in writes:
            if t.w is not None:
                add(*t.w)
            for k, v in t.r.items():
                add(k, v)
        return d

    def _emit_waits(self, eng, d):
        st = self.streams[eng]
        wd = self.waited[eng]
        for k, v in d.items():
            if k == "pe" and eng == "pe":
                continue
            if wd.get(k, 0) < v:
                wd[k] = v
                st.append(("wait", k, v))

    def _commit(self, key, reads, writes):
        reads = _flat(reads)
        writes = _flat(writes)
        self.cnt[key] = self.cnt.get(key, 0) + 1
        v = self.cnt[key]
        for t in writes:
            t.w = (key, v)
            t.r = {}
        for t in reads:
            if t.r.get(key, 0) < v:
                t.r[key] = v
        return (key, v)

    def op(self, eng, fn, reads=(), writes=()):
        self._emit_waits(eng, self._deps(reads, writes, eng))
        self.streams[eng].append(("op", fn, eng, 1))
        return self._commit(eng, reads, writes)

    def group(self, eng, fns, reads=(), writes=()):
        self._emit_waits(eng, self._deps(reads, writes, eng))
        for f in fns[:-1]:
            self.streams[eng].append(("op", f, None, 0))
        self.streams[eng].append(("op", fns[-1], eng, 1))
        return self._commit(eng, reads, writes)

    def dma(self, eng, fn, semkey, reads=(), writes=()):
        self._emit_waits(eng, self._deps(reads, writes, eng))
        self.streams[eng].append(("op", fn, semkey, 16))
        self.cnt[semkey] = self.cnt.get(semkey, 0) + 15
        return self._commit(semkey, reads, writes)

    def final_wait(self, eng, keys):
        d = {k: self.cnt[k] for k in keys if self.cnt.get(k, 0) > 0}
        self._emit_waits(eng, d)


def build_nc(NSEQ=4, NTB=4, LAYERS=(0, 1), NL=2, STAGE="full"):
    S = NTB * 512
    NCH = S // 128
    NQB = S // 256
    nc = bass.Bass("TRN2", target_bir_lowering=False)
    P = Prog()

    def din(name, shape, dt=F32):
        return nc.dram_tensor(name, list(shape), dt, kind="ExternalInput").ap()

    xT = din("xT", [NSEQ, D, S])
    outT = nc.dram_tensor("outT", [NSEQ, D, S], F32, kind="ExternalOutput").ap()
    w_up = din("w_up", [NL, 2, NJ, 128, 2048])
    w_dn = din("w_dn", [NL, 2, 2, KC, 128, NJH * 128])
    w_in = din("w_in", [NL, 8, 128, 2048])
    w_fsv = din("w_fsv", [NL, 128, SLOT])
    w_v = din("w_v", [NL, 2, 128, 2048])
    w_o = din("w_o", [NL, 4, 128, 2048])
    c_lnp = din("c_lnp", [128, NL * 6 * KC])
    c_convw = din("c_convw", [128, NL * 3 * 2])
    c_sgb = din("c_sgb", [128, NL * 2 * 256])
    c_bf = din("c_bf", [128, NL * NH])
    c_wst = din("c_wst", [128, NL * 4 * 128])
    c_bsb = din("c_bsb", [128, NL * 2 * 128])
    c_tri = din("c_tri", [128, 128])

    from contextlib import ExitStack
    es = ExitStack()

    def sb(name, shape, dt):
        return es.enter_context(nc.sbuf_tensor(name, list(shape), dt))

    with es:
        x32_t = sb("x32", [128, KC, S], F32)
        xbf_t = sb("xbf", [128, KC, S], BF16)
        hb_t = sb("hb", [128, 12, S], BF16)
        wr_t = sb("wring", [128, 4, SLOT], BF16)
        sg_t = sb("sgr", [128, 3, 512], BF16)
        tmp_t = sb("tmp", [128, 7, 512], F32)
        z_t = sb("zc", [128, 2, 516], F32)
        er_t = sb("ering", [128, 4, 512], BF16)
        rec_t = sb("rec", [128, 1, 512], F32)
        vsg_t = sb("vsg", [128, 2, 256], BF16)
        lnp_t = sb("lnp", [128, NL * 6 * KC], F32)
        convw_t = sb("convw", [128, NL * 3 * 2], F32)
        sgb_t = sb("sgb", [128, NL * 2 * 256], F32)
        bf_t = sb("bfb", [128, NL * NH], F32)
        wst_t = sb("wst", [128, NL * 4 * 128], BF16)
        bsb_t = sb("bsb", [128, NL * 2 * 128], F32)
        tri_t = sb("tri", [128, 128], F32)
        trib_t = sb("trib", [128, 128], BF16)
        ones32_t = sb("ones32", [128, 128], F32)
        onesb_t = sb("onesb", [128, 128], BF16)
        zf_t = sb("zf", [128, NCH * NH], F32)
        lf_t = sb("lf", [128, 4, NCH * NH], F32)
        offs_t = sb("offs", [128, (NCH + 1) * NH], F32)
        cref_t = sb("cref", [128, 2, NTB * NH], F32)
        cum_t = sb("cum", [128, NCH * NH], F32)
        bias_t = sb("bias", [128, 2 * NTB * (NTB + 1) * NH], F32)
        st6_t = sb("st6", [128, 4, 8, 4], F32)
        ps_t = [es.enter_context(nc.psum_tensor(f"ps{i}", [128, 512], F32)) for i in range(8)]

        sems = {}

        def sem(key):
            if key not in sems:
                sems[key] = es.enter_context(nc.semaphore("s_" + key))
            return sems[key]

        x32 = [[Tile(x32_t[:, c, tb * 512:(tb + 1) * 512]) for tb in range(NTB)] for c in range(KC)]
        xbfn = [[Tile(xbf_t[:, c, n * 128:(n + 1) * 128]) for n in range(NCH)] for c in range(KC)]
        xbf = [[MultiTile(xbf_t[:, c, tb * 512:(tb + 1) * 512], xbfn[c][4 * tb:4 * tb + 4]) for tb in range(NTB)] for c in range(KC)]
        hb = [[Tile(hb_t[:, r, i * 256:(i + 1) * 256]) for i in range(NQB)] for r in range(12)]

        def hb_ap(r, tb):
            return hb_t[:, r, tb * 512:(tb + 1) * 512]

        def hb_tl(r, tb):
            return [hb[r][2 * tb], hb[r][2 * tb + 1]]

        wslot = [Tile(wr_t[:, i, :]) for i in range(4)]
        sgr = [Tile(sg_t[:, i, :]) for i in range(3)]
        gvt = [Tile(tmp_t[:, i // 2, (i % 2) * 256:(i % 2 + 1) * 256]) for i in range(8)]
        tmp = [MultiTile(tmp_t[:, i, :], gvt[2 * i:2 * i + 2]) for i in range(4)] + [Tile(tmp_t[:, i, :]) for i in range(4, 7)]
        zc = [Tile(z_t[:, i, :]) for i in range(2)]
        ering = [Tile(er_t[:, i, :]) for i in range(4)]
        recr = [Tile(rec_t[:, i, :]) for i in range(1)]
        vsgr = [Tile(vsg_t[:, i, :]) for i in range(2)]
        st6 = [Tile(st6_t[:, i, :, :]) for i in range(4)]
        ps = [Tile(ps_t[i][:, :], excl=True) for i in range(8)]
        lnp = Tile(lnp_t[:, :]); convw = Tile(convw_t[:, :]); sgb = Tile(sgb_t[:, :]); bfb = Tile(bf_t[:, :])
        wst = Tile(wst_t[:, :]); bsb = Tile(bsb_t[:, :])
        tri = Tile(tri_t[:, :]); trib = Tile(trib_t[:, :]); ones32 = Tile(ones32_t[:, :]); onesb = Tile(onesb_t[:, :])
        zf = Tile(zf_t[:, :]); lf = [Tile(lf_t[:, i, :]) for i in range(4)]
        offs = Tile(offs_t[:, :]); cum = Tile(cum_t[:, :])
        biasT = {(qp, kc): Tile(bias_t[:, (2 * qp * (qp + 1) + kc) * NH:(2 * qp * (qp + 1) + kc + 1) * NH]) for qp in range(NTB) for kc in range(4 * qp + 4)}
        cref = Tile(cref_t[:, :, :])

        class Ring:
            def __init__(self, tiles):
                self.tiles = tiles
                self.i = 0

            def next(self):
                t = self.tiles[self.i % len(self.tiles)]
                self.i += 1
                return t

        psA = Ring(ps[0:4])
        psB = Ring(ps[4:6])
        psC = Ring(ps[6:8])
        psGU = Ring(ps[0:6])
        psO = Ring(ps[4:8])
        sgR = Ring(sgr); tmpR = Ring(tmp[4:7]); eR = Ring(ering); recR = Ring(recr); vsgR = Ring(vsgr); st6R = Ring(st6)
        wR = Ring(wslot)

        def mm(out, lhsT, rhs, start, stop):
            return lambda e: e.matmul(out, lhsT, rhs, start=start, stop=stop)

        def act(out, in_, func, reads, writes, bias=None, scale=None):
            kw = {}
            if bias is not None:
                kw["bias"] = bias
            if scale is not None:
                kw["scale"] = scale
            return P.op("act", lambda e: e.activation(out=out, in_=in_, func=func, **kw), reads, writes)

        def tt(out, in0, in1, op, reads, writes, eng="dve"):
            return P.op(eng, lambda e: e.tensor_tensor(out=out, in0=in0, in1=in1, op=op), reads, writes)

        def ts(out, in0, s1, s2, op0, op1, reads, writes):
            if op1 is None:
                return P.op("dve", lambda e: e.tensor_scalar(out=out, in0=in0, scalar1=s1, scalar2=None, op0=op0), reads, writes)
            return P.op("dve", lambda e: e.tensor_scalar(out=out, in0=in0, scalar1=s1, scalar2=s2, op0=op0, op1=op1), reads, writes)

        def stt(out, in0, scalar, in1, op0, op1, reads, writes):
            return P.op("dve", lambda e: e.scalar_tensor_tensor(out=out, in0=in0, scalar=scalar, in1=in1, op0=op0, op1=op1), reads, writes)

        def load_w(src_ap, F):
            t = wR.next()
            idx = wslot.index(t)
            dst = wr_t[:, idx, 0:F]
            if F % 1024 == 0:
                piece = 1024
            elif F == SLOT:
                piece = 704
            else:
                piece = 704
            for a in range(F // piece):
                dv = dst[:, a * piece:(a + 1) * piece]
                sv = src_ap[:, a * piece:(a + 1) * piece]
                P.dma("pool", lambda e, dv=dv, sv=sv: e.dma_start(out=dv, in_=sv), f"w{idx}", reads=(), writes=[t])
            return t, dst

        cnt_c = [0]

        def cload(tile_, dst_ap, src_ap):
            cnt_c[0] += 1
            P.dma("sp", lambda e: e.dma_start(out=dst_ap, in_=src_ap), f"c{cnt_c[0]}", reads=(), writes=[tile_])

        cload(lnp, lnp_t[:, :], c_lnp[:, :]); cload(convw, convw_t[:, :], c_convw[:, :])
        cload(sgb, sgb_t[:, :], c_sgb[:, :]); cload(bfb, bf_t[:, :], c_bf[:, :])
        wst32_ap = tmp_t[:, 0:2, :].rearrange('p a b -> p (a b)')
        for _i in range(2):
            cload(tmp[_i], tmp_t[:, _i, :], c_wst[:, _i * 512:(_i + 1) * 512])
        cload(bsb, bsb_t[:, :], c_bsb[:, :]); cload(tri, tri_t[:, :], c_tri[:, :])
        P.op("dve", lambda e: e.memset(ones32_t[:, :], 1.0), [], [ones32])
        P.op("dve", lambda e: e.memset(onesb_t[:, :], 1.0), [], [onesb])
        P.op("dve", lambda e: e.tensor_copy(out=trib_t[:, :], in_=tri_t[:, :]), [tri], [trib])
        for i in range(NL * 4):
            tt(wst_t[:, i * 128:(i + 1) * 128], wst32_ap[:, i * 128:(i + 1) * 128], tri_t[:, :], ALU.mult, [tmp[0], tmp[1], tri], [wst])

        def lnp_ap(l, i, c):
            o = (l * 6 + i) * KC + c
            return lnp_t[:, o:o + 1]

        def layer_norm(l, i, eps, after_tb=None):
            def stats_chunk(tb, c, S1, S2):
                act(xbf[c][tb].ap, x32[c][tb].ap, AF.Copy, [x32[c][tb]], [xbf[c][tb]])
                sq = sgR.next()
                act(sq.ap, x32[c][tb].ap, AF.Square, [x32[c][tb]], [sq])
                P.group("pe", [mm(S1.ap, onesb_t[:, :], xbf[c][tb].ap, c == 0, c == KC - 1)], [onesb, xbf[c][tb]], [S1])
                P.group("pe", [mm(S2.ap, onesb_t[:, :], sq.ap, c == 0, c == KC - 1)], [onesb, sq], [S2])

            def ab_chain(S1, S2):
                m2 = tmp[0]; var = tmp[1]; Asb = tmp[2]
                A = psB.next(); B = psB.next()
                act(m2.ap, S1.ap, AF.Square, [S1], [m2], scale=1.0 / D)
                stt(var.ap, S2.ap, 1.0 / D, m2.ap, ALU.mult, ALU.subtract, [S2, m2], [var])
                act(var.ap, var.ap, AF.Ln, [var], [var], bias=eps)
                act(Asb.ap, var.ap, AF.Exp, [var], [Asb], scale=-0.5)
                act(A.ap, Asb.ap, AF.Copy, [Asb], [A])
                stt(B.ap, S1.ap, -1.0 / D, Asb.ap, ALU.mult, ALU.mult, [S1, Asb], [B])
                return A, B

            def norm_group(tb, cs_, A, B):
                t1s = {c: tmpR.next() for c in cs_}
                for c in cs_:
                    tt(t1s[c].ap, x32[c][tb].ap, A.ap, ALU.mult, [x32[c][tb], A], [t1s[c]])
                for c in cs_:
                    tt(t1s[c].ap, t1s[c].ap, B.ap, ALU.add, [t1s[c], B], [t1s[c]])
                for c in cs_:
                    t1 = t1s[c]
                    ts(x32[c][tb].ap, t1.ap, lnp_ap(l, 2 * i, c), lnp_ap(l, 2 * i + 1, c), ALU.mult, ALU.add, [t1, lnp], [x32[c][tb]])
                    act(xbf[c][tb].ap, t1.ap, AF.Identity, [t1, lnp], [xbf[c][tb]], bias=lnp_ap(l, 2 * i + 1, c), scale=lnp_ap(l, 2 * i, c))

            Scur = (psC.next(), psC.next())
            for c in range(KC):
                stats_chunk(0, c, *Scur)
            AB = ab_chain(*Scur)
            for tb in range(NTB):
                nxt_ = tb + 1 < NTB
                if nxt_:
                    Scur = (psC.next(), psC.next())
                for c0 in range(0, KC, 3):
                    cs_ = list(range(c0, min(c0 + 3, KC)))
                    if nxt_:
                        for c in cs_:
                            stats_chunk(tb + 1, c, *Scur)
                    norm_group(tb, cs_, *AB)
                if nxt_:
                    AB = ab_chain(*Scur)
                if after_tb is not None:
                    after_tb(tb)

        def ffn_make(l, f):
            cres = 0.5 / ALPHA
            groups = [[0, 1, 2], [3, 4, 5], [6, 7, 8], [9, 10]]
            state = {}

            def up_tb(g, wt, tb, ring=psA):
                for jj in g:
                    wtile, wap = wt[jj]
                    w3 = wap.rearrange("p (k c) -> p k c", c=256)
                    G = ring.next(); U = ring.next()
                    rd = [wtile] + [xbf[kc][tb] for kc in range(KC)]
                    P.group("pe", [mm(G.ap, w3[:, kc, 0:128], xbf[kc][tb].ap, kc == 0, kc == KC - 1) for kc in range(KC)], rd, [G])
                    P.group("pe", [mm(U.ap, w3[:, kc, 128:256], xbf[kc][tb].ap, kc == 0, kc == KC - 1) for kc in range(KC)], rd, [U])
                    sg = sgR.next()
                    act(sg.ap, G.ap, AF.Silu, [G], [sg])
                    tt(hb_ap(jj, tb), sg.ap, U.ap, ALU.mult, [sg, U], hb_tl(jj, tb))

            def head(tb):
                if "wt" not in state:
                    state["wt"] = {jj: load_w(w_up[l, f, jj], 2048) for jj in groups[0]}
                up_tb(groups[0], state["wt"], tb)

            def rest():
                for hf in range(2):
                    for gi, g in enumerate(groups):
                        if hf == 0 and gi == 0:
                            continue
                        wt = {jj: load_w(w_up[l, f, hf * NJH + jj], 2048) for jj in g}
                        for tb in range(NTB):
                            up_tb(g, wt, tb, psGU)
                    for m in range(KC):
                        wtile, wap = load_w(w_dn[l, f, hf, m], NJH * 128)
                        w3 = wap.rearrange("p (j c) -> p j c", c=128)
                        for tb in range(NTB):
                            Y = psB.next()
                            rd = [wtile]
                            for jj in range(NJH):
                                rd += hb_tl(jj, tb)
                            P.group("pe", [mm(Y.ap, w3[:, jj, :], hb_ap(jj, tb), jj == 0, jj == NJH - 1) for jj in range(NJH)], rd, [Y])
                            stt(x32[m][tb].ap, Y.ap, cres, x32[m][tb].ap, ALU.mult, ALU.add, [Y, x32[m][tb]], [x32[m][tb]])

            return head, rest

        ROW_Q, ROW_K, ROW_U, ROW_YA = 0, 4, 8, 10

        def proj_fm(wtile, w3, c2, tb):
            Pp = psA.next()
            rd = [wtile] + [xbf[kc][tb] for kc in range(KC)]
            P.group("pe", [mm(Pp.ap, w3[:, kc, c2 * 128:(c2 + 1) * 128], xbf[kc][tb].ap, kc == 0, kc == KC - 1) for kc in range(KC)], rd, [Pp])
            return Pp

        def mixer_make(l):
            state = {}

            def conv_tb(tb):
                if "W" not in state:
                    WA = load_w(w_in[l, 0], 2048); WB = load_w(w_in[l, 1], 2048); WC = load_w(w_in[l, 2], 2048)
                    state["W"] = (WA, WB, WC)
                    for c in range(2):
                        P.op("dve", lambda e, c=c: e.memset(z_t[:, c, 0:2], 0.0), [], [zc[c]])
                WA, WB, WC = state["W"]
                WA3 = WA[1].rearrange("p (k c) -> p k c", c=256)
                WB3 = WB[1].rearrange("p (k c) -> p k c", c=256)
                WC3 = WC[1].rearrange("p (k c) -> p k c", c=256)
                for c in range(2):
                    def cw(i):
                        o = (l * 3 + i) * 2 + c
                        return convw_t[:, o:o + 1]
                    P1 = proj_fm(WA[0], WA3, c, tb)
                    cct = tmpR.next()
                    act(cct.ap, P1.ap, AF.Copy, [P1], [cct])
                    P2 = proj_fm(WB[0], WB3, c, tb)
                    tt(z_t[:, c, 2:514], P2.ap, cct.ap, ALU.mult, [P2, cct], [zc[c]])
                    yv = tmpR.next()
                    ts(yv.ap, z_t[:, c, 2:514], cw(2), None, ALU.mult, None, [zc[c], convw], [yv])
                    stt(yv.ap, z_t[:, c, 1:513], cw(1), yv.ap, ALU.mult, ALU.add, [zc[c], convw, yv], [yv])
                    stt(yv.ap, z_t[:, c, 0:512], cw(0), yv.ap, ALU.mult, ALU.add, [zc[c], convw, yv], [yv])
                    P3 = proj_fm(WC[0], WC3, c, tb)
                    tt(hb_ap(ROW_YA + c, tb), P3.ap, yv.ap, ALU.mult, [P3, yv], hb_tl(ROW_YA + c, tb))
                    P.op("dve", lambda e, c=c: e.tensor_copy(out=z_t[:, c, 0:2], in_=z_t[:, c, 512:514]), [zc[c]], [zc[c]])

            def rest():
                mixer_rest(l)

            return conv_tb, rest

        def mixer_rest(l):
            if STAGE == 'm1a':
                return
            for ti, row0, kind in ((3, ROW_Q, "c"), (4, ROW_Q + 2, "c"), (5, ROW_K, "c"), (6, ROW_K + 2, "c"), (7, ROW_U, "g")):
                W = load_w(w_in[l, ti], 2048)
                W3 = W[1].rearrange("p (k c) -> p k c", c=256)
                for tb in range(NTB):
                    for c2 in range(2):
                        Pp = proj_fm(W[0], W3, c2, tb)
                        fn = AF.Copy if kind == "c" else AF.Gelu_apprx_tanh
                        act(hb_ap(row0 + c2, tb), Pp.ap, fn, [Pp], hb_tl(row0 + c2, tb))
            if STAGE == 'm1b':
                return
            WF = load_w(w_fsv[l], SLOT); WV0 = load_w(w_v[l, 0], 2048); WV1 = load_w(w_v[l, 1], 2048)
            WF3 = WF[1].rearrange("p (k c) -> p k c", c=264)
            WV3 = [WV0[1].rearrange("p (k c) -> p k c", c=512), WV1[1].rearrange("p (k c) -> p k c", c=512)]
            GN = 4
            grp_s6 = {}

            def phaseA(gi):
                s6g = st6R.next()
                grp_s6[gi] = s6g
                for j in range(GN):
                    n = gi * GN + j
                    tb = n // 4
                    cs = slice((n % 4) * 128, (n % 4) * 128 + 128)
                    xs = [xbfn[kc][n] for kc in range(KC)]
                    T1 = psA.next()
                    P.group("pe", [mm(T1.ap[:, 0:264], xbf[kc][tb].ap[:, cs], WF3[:, kc, :], kc == 0, kc == KC - 1) for kc in range(KC)], [WF[0]] + xs, [T1])
                    T2 = psA.next()
                    fns = [mm(T2.ap[:, 0:512], xbf[kc][tb].ap[:, cs], WV3[kc // 4][:, kc % 4, :], kc == 0, kc == KC - 1) for kc in range(KC)]
                    P.group("pe", fns, [WV0[0], WV1[0]] + xs, [T2])
                    tt(zf_t[:, n * NH:(n + 1) * NH], T1.ap[:, 256:264], bf_t[:, l * NH:(l + 1) * NH], ALU.add, [T1, bfb], [zf])
                    gv = gvt[(gi % 2) * GN + j]
                    P.op("act", lambda e, gv=gv, T1=T1, j=j: e.activation(out=gv.ap, in_=T1.ap[:, 0:256], func=AF.Gelu_apprx_tanh, accum_out=s6g.ap[:, 0, j:j + 1]), [T1], [gv, s6g])
                    P.op("act", lambda e, gv=gv, j=j: e.activation(out=z_t[:, 0, 0:256], in_=gv.ap, func=AF.Square, accum_out=s6g.ap[:, 1, j:j + 1]), [gv], [zc[0], s6g])
                    T2v = T2.ap.rearrange("p (h d) -> p h d", d=64)
                    c0 = n * 128
                    P.op("act", lambda e, T2v=T2v, c0=c0: e.activation(out=xbf_t[:, 0:8:2, c0:c0 + 64], in_=T2v[:, 0:8:2, :], func=AF.Copy), [T2] + xs, xs)
                    P.op("act", lambda e, T2v=T2v, c0=c0: e.activation(out=xbf_t[:, 1:8:2, c0 + 64:c0 + 128], in_=T2v[:, 1:8:2, :], func=AF.Copy), [T2] + xs, xs)
                    P.op("dve", lambda e, c0=c0: e.memset(xbf_t[:, 0:8:2, c0 + 64:c0 + 128], 1.0), xs, xs)
                    P.op("dve", lambda e, c0=c0: e.memset(xbf_t[:, 1:8:2, c0:c0 + 64], 1.0), xs, xs)

            def phaseB(gi):
                s6g = grp_s6[gi]
                f = lambda k: s6g.ap[:, k, :]
                ts(f(2), f(0), 1.0 / 256, None, ALU.mult, None, [s6g], [s6g])
                tt(f(3), f(2), f(2), ALU.mult, [s6g], [s6g])
                stt(f(4), f(1), 1.0 / 256, f(3), ALU.mult, ALU.subtract, [s6g], [s6g])
                act(f(4), f(4), AF.Ln, [s6g], [s6g], bias=LN_EPS)
                act(f(5), f(4), AF.Exp, [s6g], [s6g], scale=-0.5)
                stt(f(6), f(2), -1.0, f(5), ALU.mult, ALU.mult, [s6g], [s6g])

            def phaseC(gi):
                s6g = grp_s6.pop(gi)
                for j in range(GN):
                    n = gi * GN + j
                    gv = gvt[(gi % 2) * GN + j]
                    act(gv.ap, gv.ap, AF.Identity, [gv, s6g], [gv], bias=s6g.ap[:, 6, j:j + 1], scale=s6g.ap[:, 5, j:j + 1])
                    tt(gv.ap, gv.ap, sgb_t[:, (l * 2) * 256:(l * 2 + 1) * 256], ALU.mult, [gv, sgb], [gv])
                    vs = vsgR.next()
                    tt(vs.ap, gv.ap, sgb_t[:, (l * 2 + 1) * 256:(l * 2 + 2) * 256], ALU.add, [gv, sgb], [vs])
                    for cp in range(2):
                        Mx = psC.next()
                        fns = []
                        for gg in range(2):
                            g4 = 2 * cp + gg
                            o = (l * 4 + g4) * 128
                            fns.append(mm(Mx.ap[gg * 64:(gg + 1) * 64, 0:128], vs.ap[:, g4 * 64:(g4 + 1) * 64], wst_t[:, o:o + 128], True, True))
                        P.group("pe", fns, [vs, wst], [Mx])
                        t1 = tmpR.next()
                        ob = (l * 2 + cp) * 128
                        tt(t1.ap[:, 0:128], Mx.ap[:, 0:128], bsb_t[:, ob:ob + 128], ALU.add, [Mx, bsb], [t1])
                        utl = hb[ROW_U + cp][n // 2]
                        uap = hb_t[:, ROW_U + cp, n * 128:(n + 1) * 128]
                        tt(uap, t1.ap[:, 0:128], uap, ALU.mult, [t1, utl], [utl])

            def m2_chain():
                NF = NCH * NH
                mn, ab, ex, l1 = lf
                ts(mn.ap, zf_t[:, :], 0.0, None, ALU.min, None, [zf], [mn])
                stt(ab.ap, mn.ap, 2.0, zf_t[:, :], ALU.mult, ALU.subtract, [mn, zf], [ab])
                act(ex.ap, ab.ap, AF.Exp, [ab], [ex])
                act(l1.ap, ex.ap, AF.Ln, [ex], [l1], bias=1.0)
                tt(mn.ap, mn.ap, l1.ap, ALU.subtract, [mn, l1], [mn])
                CW = psB.next(); TOT = psB.next()
                P.group("pe", [mm(CW.ap[:, 0:NF], tri_t[:, :], mn.ap, True, True)], [tri, mn], [CW])
                P.group("pe", [mm(TOT.ap[:, 0:NF], ones32_t[:, :], mn.ap, True, True)], [ones32, mn], [TOT])
                P.op("dve", lambda e: e.memset(offs_t[:, 0:NH], 0.0), [], [offs])
                for n in range(1, NCH + 1):
                    tt(offs_t[:, n * NH:(n + 1) * NH], offs_t[:, (n - 1) * NH:n * NH], TOT.ap[:, (n - 1) * NH:n * NH], ALU.add, [offs, TOT], [offs])
                tt(cum_t[:, :], CW.ap[:, 0:NF], offs_t[:, 0:NF], ALU.add, [CW, offs], [cum])
                for qp in range(NTB):
                    o0 = 4 * qp * NH
                    o1 = (4 * qp + 4) * NH
                    dq = cref_t[:, 0, qp * NH:(qp + 1) * NH]
                    cr = cref_t[:, 1, qp * NH:(qp + 1) * NH]
                    tt(dq, offs_t[:, o0:o0 + NH], offs_t[:, o1:o1 + NH], ALU.subtract, [offs], [cref])
                    ts(dq, dq, -60.0, 0.0, ALU.add, ALU.max, [cref], [cref])
                    tt(cr, offs_t[:, o0:o0 + NH], dq, ALU.subtract, [offs, cref], [cref])
                for qp in range(NTB):
                    cr = cref_t[:, 1, qp * NH:(qp + 1) * NH]
                    for kc in range(4 * qp + 4):
                        o = (2 * qp * (qp + 1) + kc) * NH
                        tt(bias_t[:, o:o + NH], cr, cum_t[:, kc * NH:(kc + 1) * NH], ALU.subtract, [cref, cum], [biasT[(qp, kc)]])

            NG = NCH // GN
            for gi in range(NG + 1):
                if gi < NG:
                    phaseA(gi)
                if gi == NG - 1 and STAGE not in ('m1c',):
                    m2_chain()
                if gi >= 1:
                    phaseC(gi - 1)
                if gi < NG:
                    phaseB(gi)
            if STAGE == 'm1c':
                return
            if STAGE == 'm2':
                return
            NQP = NQB // 2
            steps = []
            for c in range(NH // 2):
                for qp in range(NQP):
                    for kc in range(4 * qp + 4):
                        steps.append((c, qp, kc))
            Ocur = {}

            def emit_score(c, qp, kc):
                qts = [hb[ROW_Q + c][2 * qp], hb[ROW_Q + c][2 * qp + 1]]
                kt = hb[ROW_K + c][kc // 2]
                st0 = max(0, kc - 4 * qp) * 128
                STs = [psA.next(), psA.next()]
                Es = [eR.next(), eR.next()]
                fns = []
                for hp in range(2):
                    ps_ = slice(hp * 64, hp * 64 + 64)
                    kap = hb_t[ps_, ROW_K + c, kc * 128:(kc + 1) * 128]
                    qap = hb_t[ps_, ROW_Q + c, qp * 512 + st0:(qp + 1) * 512]
                    fns.append(mm(STs[hp].ap[:, st0:512], kap, qap, True, True))
                P.group("pe", fns, [kt] + qts, STs)
                for hp in range(2):
                    h = 2 * c + hp
                    ST = STs[hp]; E = Es[hp]
                    o = (2 * qp * (qp + 1) + kc) * NH + h
                    act(E.ap[:, st0:512], ST.ap[:, st0:512], AF.Exp, [ST, biasT[(qp, kc)]], [E], bias=bias_t[:, o:o + 1], scale=0.125)
                    if kc >= 4 * qp:
                        tt(E.ap[:, st0:st0 + 128], E.ap[:, st0:st0 + 128], trib_t[:, :], ALU.mult, [E, trib], [E])
                return Es

            def emit_pv(c, qp, kc, Es):
                nk = 4 * qp + 4
                st0 = max(0, kc - 4 * qp) * 128
                if kc == 0:
                    Ocur[(c, qp)] = [psO.next(), psO.next()]
                Os = Ocur[(c, qp)]
                fns = []
                rd = []
                for hp in range(2):
                    h = 2 * c + hp
                    rd += [xbfn[h][kc], Es[hp]]
                    vap = xbf_t[:, h, kc * 128:(kc + 1) * 128]
                    fns.append(mm(Os[hp].ap[:, st0:512], vap, Es[hp].ap[:, st0:512], kc == 0, kc == nk - 1))
                P.group("pe", fns, rd, Os)
                if kc == nk - 1:
                    qts = [hb[ROW_Q + c][2 * qp], hb[ROW_Q + c][2 * qp + 1]]
                    rc = recR.next()
                    for hp in range(2):
                        O = Os[hp]
                        ps_ = slice(hp * 64, hp * 64 + 64)
                        dn = slice(64, 128) if hp == 0 else slice(0, 64)
                        P.op("dve", lambda e, rc=rc, O=O, dn=dn, ps_=ps_: e.reciprocal(out=rc.ap[ps_, :], in_=O.ap[dn, :]), [O], [rc])
                        yap = hb_t[ps_, ROW_Q + c, qp * 512:(qp + 1) * 512]
                        tt(yap, O.ap[ps_, :], rc.ap[ps_, :], ALU.mult, [O, rc], qts)

            pend = []
            for st_ in steps:
                if len(pend) == 2:
                    st0_, E0 = pend.pop(0)
                    emit_pv(*st0_, E0)
                Es_ = emit_score(*st_)
                pend.append((st_, Es_))
            for st0_, E0 in pend:
                emit_pv(*st0_, E0)
            if STAGE == 'm3':
                return
            yrows = [ROW_YA, ROW_YA + 1, ROW_Q, ROW_Q + 1, ROW_Q + 2, ROW_Q + 3, ROW_U, ROW_U + 1]
            for mp in range(4):
                W = load_w(w_o[l, mp], 2048)
                W3 = W[1].rearrange("p (k c) -> p k c", c=256)
                for tb in range(NTB):
                    for m2 in range(2):
                        m = mp * 2 + m2
                        Y = psB.next()
                        rd = [W[0]]
                        for r in yrows:
                            rd += hb_tl(r, tb)
                        P.group("pe", [mm(Y.ap, W3[:, kc, m2 * 128:(m2 + 1) * 128], hb_ap(yrows[kc], tb), kc == 0, kc == KC - 1) for kc in range(KC)], rd, [Y])
                        stt(x32[m][tb].ap, Y.ap, 1.0 / ALPHA, x32[m][tb].ap, ALU.mult, ALU.add, [Y, x32[m][tb]], [x32[m][tb]])

        eps_s = LN_EPS / (ALPHA * ALPHA)
        def load_blk(s, tb):
            for c in range(KC):
                P.dma("sp", lambda e, c=c: e.dma_start(out=x32[c][tb].ap, in_=xT[s, c * 128:(c + 1) * 128, tb * 512:(tb + 1) * 512]), f"xi{c}_{tb}", reads=(), writes=[x32[c][tb]])
            for c in range(KC):
                act(xbf[c][tb].ap, x32[c][tb].ap, AF.Copy, [x32[c][tb]], [xbf[c][tb]])

        def store_blk(s, tb):
            for c in range(KC):
                P.dma("sp", lambda e, c=c: e.dma_start(out=outT[s, c * 128:(c + 1) * 128, tb * 512:(tb + 1) * 512], in_=x32[c][tb].ap), f"xo{c}_{tb}", reads=[x32[c][tb]], writes=())

        pre = None
        for s in range(NSEQ):
            if pre is None:
                for tb in range(NTB):
                    load_blk(s, tb)
                nxt = None
            else:
                nxt = pre
                pre = None
            for li, l in enumerate(LAYERS):
                if nxt is None:
                    f1_head, f1_rest = ffn_make(l, 0)
                    for tb in range(NTB):
                        f1_head(tb)
                else:
                    f1_head, f1_rest = nxt
                f1_rest()
                if STAGE == "ffn1":
                    layer_norm(l, 0, eps_s)
                    nxt = None
                    continue
                conv_tb, mix_rest = mixer_make(l)
                layer_norm(l, 0, eps_s, after_tb=conv_tb)
                mix_rest()
                if STAGE.startswith("m") and STAGE != "mix":
                    nxt = None
                    continue
                if STAGE == "mix":
                    layer_norm(l, 1, eps_s)
                    nxt = None
                    continue
                f2_head, f2_rest = ffn_make(l, 1)
                layer_norm(l, 1, eps_s, after_tb=f2_head)
                f2_rest()
                if li + 1 < len(LAYERS):
                    nxt = ffn_make(LAYERS[li + 1], 0)
                    layer_norm(l, 2, eps_s, after_tb=nxt[0])
                else:
                    nxt = None
                    if s + 1 < NSEQ:
                        pre = ffn_make(LAYERS[0], 0)

                    def io_hook(tb, s=s, pre=pre):
                        store_blk(s, tb)
                        if pre is not None:
                            load_blk(s + 1, tb)
                            pre[0](tb)

                    layer_norm(l, 2, eps_s, after_tb=io_hook)
            if STAGE != "full":
                for tb in range(NTB):
                    store_blk(s, tb)
        P.final_wait("sp", [f"xo{c}_{tb}" for c in range(KC) for tb in range(NTB)])

        for k in list(P.cnt.keys()):
            sem(k)

        def replay(engname):
            def body(e):
                for it in P.streams[engname]:
                    if it[0] == "wait":
                        e.wait_ge(sems[it[1]], it[2])
                    else:
                        ins = it[1](e)
                        if it[3]:
                            ins.then_inc(sems[it[2]], it[3])
            return body

        with nc.Block() as block:
            block.tensor(replay("pe"))
            block.scalar(replay("act"))
            block.vector(replay("dve"))
            block.gpsimd(replay("pool"))
            block.sync(replay("sp"))
    return nc


def _prep_weights(inp, NL=2):
    f32 = np.float32
    def up_tiles(w):
        a = w.reshape(NL, KC, 128, 2, NJ, 128)
        a = a.transpose(0, 4, 2, 1, 3, 5)
        return np.ascontiguousarray(a.reshape(NL, NJ, 128, 2048))
    def dn_tiles(w):
        a = w.reshape(NL, 2, NJH, 128, KC, 128)
        a = a.transpose(0, 1, 4, 3, 2, 5)
        return np.ascontiguousarray(a.reshape(NL, 2, KC, 128, NJH * 128))
    def col_tile(w, cols):
        a = w[:, :, cols].reshape(NL, KC, 128, len(cols))
        a = a.transpose(0, 2, 1, 3)
        return np.ascontiguousarray(a.reshape(NL, 128, KC * len(cols)))
    w_up = np.stack([up_tiles(inp["ffn1_w_up"]), up_tiles(inp["ffn2_w_up"])], axis=1).astype(f32)
    w_dn = np.stack([dn_tiles(inp["ffn1_w_down"]), dn_tiles(inp["ffn2_w_down"])], axis=1).astype(f32)
    wi = inp["mix_w_in"]
    r = np.arange
    tiles = [r(256, 512), r(512, 768), r(0, 256), r(768, 1024), r(1024, 1280), r(1280, 1536), r(1536, 1792), r(2312, 2568)]
    w_in = np.stack([col_tile(wi, t) for t in tiles], axis=1).astype(f32)
    w_fsv = col_tile(wi, np.concatenate([r(2568, 2824), r(2304, 2312)])).astype(f32)
    wv = wi[:, :, 1792:2304].reshape(NL, 2, 4, 128, 512).transpose(0, 1, 3, 2, 4)
    w_v = np.ascontiguousarray(wv.reshape(NL, 2, 128, 2048)).astype(f32)
    wo = inp["mix_w_out"]
    w_o = np.stack([col_tile(wo, r(mp * 256, (mp + 1) * 256)) for mp in range(4)], axis=1).astype(f32)
    lnl = [inp["ln1_g"], inp["ln1_b"], inp["ln2_g"], inp["ln2_b"], inp["ln3_g"], inp["ln3_b"]]
    lnp = np.stack(lnl, axis=1).reshape(NL, 6, KC, 128).transpose(3, 0, 1, 2).reshape(128, NL * 6 * KC)
    convw = inp["conv_w"].reshape(NL, 3, 2, 128).transpose(3, 0, 1, 2).reshape(128, NL * 3 * 2)
    sgb = np.stack([inp["sgu_ln_g"], inp["sgu_ln_b"]], axis=1).reshape(1, NL * 2 * 256)
    sgb = np.broadcast_to(sgb, (128, NL * 2 * 256))
    bfb = np.broadcast_to(inp["fox_b_f"].reshape(1, NL * NH), (128, NL * NH))
    wst = inp["sgu_w_s"].transpose(3, 0, 1, 2).reshape(128, NL * 4 * 128)
    bs = inp["sgu_b_s"].reshape(NL, 2, 2, 1, 128)
    bsb = np.broadcast_to(bs, (NL, 2, 2, 64, 128)).transpose(2, 3, 0, 1, 4).reshape(128, NL * 2 * 128)
    tri = (np.arange(128)[None, :] >= np.arange(128)[:, None]).astype(f32)
    c = lambda a: np.ascontiguousarray(a, dtype=f32)
    return dict(w_up=w_up, w_dn=w_dn, w_in=w_in, w_fsv=w_fsv, w_v=w_v, w_o=w_o, c_lnp=c(lnp), c_convw=c(convw),
                c_sgb=c(sgb), c_bf=c(bfb), c_wst=c(wst), c_bsb=c(bsb), c_tri=c(tri))


def kernel(**inputs):
    inp = {k: np.asarray(v) for k, v in inputs.items()}
    x = inp["x"]
    B, S, _ = x.shape
    ncores = 8
    per = B // ncores
    wd = _prep_weights(inp)
    nc = build_nc(NSEQ=per, NTB=S // 512)
    in_maps = []
    for i in range(ncores):
        m = dict(wd)
        m["xT"] = np.ascontiguousarray(x[i * per:(i + 1) * per].transpose(0, 2, 1))
        in_maps.append(m)
    res = run_bass_kernel_spmd(nc, in_maps, core_ids=list(range(ncores)))
    outs = [np.asarray(r["outT"]).transpose(0, 2, 1) for r in res.results]
    return np.ascontiguousarray(np.concatenate(outs, axis=0)).astype(np.float32)
```

```python
import numpy as np
import os
DBG = os.environ.get('KDBG', '')
KSTEP = int(os.environ.get('KSTEP', '99'))
import concourse.bass as bass
import concourse.mybir as mybir
from concourse.bass_utils import run_bass_kernel_spmd

F32 = mybir.dt.float32
BF16 = mybir.dt.bfloat16
AF = mybir.ActivationFunctionType
ALU = mybir.AluOpType

D = 1024
KC = 8
DFF = 2816
NJ = 22
NJH = 11
DEPTH = 2
ALPHA = (2 * DEPTH) ** 0.25
LN_EPS = 1e-5
NH = 8
SLOT = 2112

ENG = ("pe", "act", "dve", "pool", "sp")


class Tile:
    __slots__ = ("ap", "w", "r", "name", "excl")

    def __init__(self, ap, name="", excl=False):
        self.excl = excl
        self.ap = ap
        self.w = None
        self.r = {}
        self.name = name


class MultiTile:
    __slots__ = ("ap", "subs")

    def __init__(self, ap, subs):
        self.ap = ap
        self.subs = subs


def _flat(ts_):
    out = []
    for t in ts_:
        if isinstance(t, MultiTile):
            out.extend(t.subs)
        else:
            out.append(t)
    return out


class Prog:
    def __init__(self):
        self.streams = {e: [] for e in ENG}
        self.cnt = {}
        self.waited = {e: {} for e in ENG}

    def _deps(self, reads, writes, eng=None):
        d = {}
        reads = _flat(reads)
        writes = _flat(writes)

        def add(k, v):
            if d.get(k, 0) < v:
                d[k] = v

        for t in reads:
            if t.w is not None:
                add(*t.w)
            if t.excl:
                for k, v in t.r.items():
                    if k != eng:
                        add(k, v)
        for t in writes:
            if t.w is not None:
                add(*t.w)
            for k, v in t.r.items():
                add(k, v)
        return d

    def _emit_waits(self, eng, d):
        st = self.streams[eng]
        wd = self.waited[eng]
        for k, v in d.items():
            if k == "pe" and eng == "pe":
                continue
            if wd.get(k, 0) < v:
                wd[k] = v
                st.append(("wait", k, v))

    def _commit(self, key, reads, writes):
        reads = _flat(reads)
        writes = _flat(writes)
        self.cnt[key] = self.cnt.get(key, 0) + 1
        v = self.cnt[key]
        for t in writes:
            t.w = (key, v)
            t.r = {}
        for t in reads:
            if t.r.get(key, 0) < v:
                t.r[key] = v
        return (key, v)

    def op(self, eng, fn, reads=(), writes=()):
        self._emit_waits(eng, self._deps(reads, writes, eng))
        self.streams[eng].append(("op", fn, eng, 1))
        return self._commit(eng, reads, writes)

    def group(self, eng, fns, reads=(), writes=()):
        self._emit_waits(eng, self._deps(reads, writes, eng))
        for f in fns[:-1]:
            self.streams[eng].append(("op", f, None, 0))
        self.streams[eng].append(("op", fns[-1], eng, 1))
        return self._commit(eng, reads, writes)

    def dma(self, eng, fn, semkey, reads=(), writes=()):
        self._emit_waits(eng, self._deps(reads, writes, eng))
        self.streams[eng].append(("op", fn, semkey, 16))
        self.cnt[semkey] = self.cnt.get(semkey, 0) + 15
        return self._commit(semkey, reads, writes)

    def final_wait(self, eng, keys):
        d = {k: self.cnt[k] for k in keys if self.cnt.get(k, 0) > 0}
        self._emit_waits(eng, d)


def build_nc(NSEQ=4, NTB=4, LAYERS=(0, 1), NL=2, STAGE="full"):
    S = NTB * 512
    NCH = S // 128
    NQB = S // 256
    nc = bass.Bass("TRN2", target_bir_lowering=False)
    P = Prog()

    def din(name, shape, dt=F32):
        return nc.dram_tensor(name, list(shape), dt, kind="ExternalInput").ap()

    xT = din("xT", [NSEQ, D, S])
    outT = nc.dram_tensor("outT", [NSEQ, D, S], F32, kind="ExternalOutput").ap()
    w_up = din("w_up", [NL, 2, NJ, 128, 2048])
    w_dn = din("w_dn", [NL, 2, 2, KC, 128, NJH * 128])
    w_in = din("w_in", [NL, 8, 128, 2048])
    w_fsv = din("w_fsv", [NL, 128, SLOT])
    w_v = din("w_v", [NL, 2, 128, 2048])
    w_o = din("w_o", [NL, 4, 128, 2048])
    c_lnp = din("c_lnp", [128, NL * 6 * KC])
    c_convw = din("c_convw", [128, NL * 3 * 2])
    c_sgb = din("c_sgb", [128, NL * 2 * 256])
    c_bf = din("c_bf", [128, NL * NH])
    c_wst = din("c_wst", [128, NL * 4 * 128])
    c_bsb = din("c_bsb", [128, NL * 2 * 128])
    c_tri = din("c_tri", [128, 128])

    from contextlib import ExitStack
    es = ExitStack()

    def sb(name, shape, dt):
        return es.enter_context(nc.sbuf_tensor(name, list(shape), dt))

    with es:
        x32_t = sb("x32", [128, KC, S], F32)
        xbf_t = sb("xbf", [128, KC, S], BF16)
        hb_t = sb("hb", [128, 12, S], BF16)
        wr_t = sb("wring", [128, 4, SLOT], BF16)
        sg_t = sb("sgr", [128, 3, 512], BF16)
        tmp_t = sb("tmp", [128, 7, 512], F32)
        z_t = sb("zc", [128, 2, 516], F32)
        er_t = sb("ering", [128, 4, 512], BF16)
        rec_t = sb("rec", [128, 1, 512], F32)
        vsg_t = sb("vsg", [128, 2, 256], BF16)
        lnp_t = sb("lnp", [128, NL * 6 * KC], F32)
        convw_t = sb("convw", [128, NL * 3 * 2], F32)
        sgb_t = sb("sgb", [128, NL * 2 * 256], F32)
        bf_t = sb("bfb", [128, NL * NH], F32)
        wst_t = sb("wst", [128, NL * 4 * 128], BF16)
        bsb_t = sb("bsb", [128, NL * 2 * 128], F32)
        tri_t = sb("tri", [128, 128], F32)
        trib_t = sb("trib", [128, 128], BF16)
        ones32_t = sb("ones32", [128, 128], F32)
        onesb_t = sb("onesb", [128, 128], BF16)
        zf_t = sb("zf", [128, NCH * NH], F32)
        lf_t = sb("lf", [128, 4, NCH * NH], F32)
        offs_t = sb("offs", [128, (NCH + 1) * NH], F32)
        cref_t = sb("cref", [128, 2, NTB * NH], F32)
        cum_t = sb("cum", [128, NCH * NH], F32)
        bias_t = sb("bias", [128, 2 * NTB * (NTB + 1) * NH], F32)
        st6_t = sb("st6", [128, 4, 8, 4], F32)
        ps_t = [es.enter_context(nc.psum_tensor(f"ps{i}", [128, 512], F32)) for i in range(8)]

        sems = {}

        def sem(key):
            if key not in sems:
                sems[key] = es.enter_context(nc.semaphore("s_" + key))
            return sems[key]

        x32 = [[Tile(x32_t[:, c, tb * 512:(tb + 1) * 512]) for tb in range(NTB)] for c in range(KC)]
        xbfn = [[Tile(xbf_t[:, c, n * 128:(n + 1) * 128]) for n in range(NCH)] for c in range(KC)]
        xbf = [[MultiTile(xbf_t[:, c, tb * 512:(tb + 1) * 512], xbfn[c][4 * tb:4 * tb + 4]) for tb in range(NTB)] for c in range(KC)]
        hb = [[Tile(hb_t[:, r, i * 256:(i + 1) * 256]) for i in range(NQB)] for r in range(12)]

        def hb_ap(r, tb):
            return hb_t[:, r, tb * 512:(tb + 1) * 512]

        def hb_tl(r, tb):
            return [hb[r][2 * tb], hb[r][2 * tb + 1]]

        wslot = [Tile(wr_t[:, i, :]) for i in range(4)]
        sgr = [Tile(sg_t[:, i, :]) for i in range(3)]
        gvt = [Tile(tmp_t[:, i // 2, (i % 2) * 256:(i % 2 + 1) * 256]) for i in range(8)]
        tmp = [MultiTile(tmp_t[:, i, :], gvt[2 * i:2 * i + 2]) for i in range(4)] + [Tile(tmp_t[:, i, :]) for i in range(4, 7)]
        zc = [Tile(z_t[:, i, :]) for i in range(2)]
        ering = [Tile(er_t[:, i, :]) for i in range(4)]
        recr = [Tile(rec_t[:, i, :]) for i in range(1)]
        vsgr = [Tile(vsg_t[:, i, :]) for i in range(2)]
        st6 = [Tile(st6_t[:, i, :, :]) for i in range(4)]
        ps = [Tile(ps_t[i][:, :], excl=True) for i in range(8)]
        lnp = Tile(lnp_t[:, :]); convw = Tile(convw_t[:, :]); sgb = Tile(sgb_t[:, :]); bfb = Tile(bf_t[:, :])
        wst = Tile(wst_t[:, :]); bsb = Tile(bsb_t[:, :])
        tri = Tile(tri_t[:, :]); trib = Tile(trib_t[:, :]); ones32 = Tile(ones32_t[:, :]); onesb = Tile(onesb_t[:, :])
        zf = Tile(zf_t[:, :]); lf = [Tile(lf_t[:, i, :]) for i in range(4)]
        offs = Tile(offs_t[:, :]); cum = Tile(cum_t[:, :])
        biasT = {(qp, kc): Tile(bias_t[:, (2 * qp * (qp + 1) + kc) * NH:(2 * qp * (qp + 1) + kc + 1) * NH]) for qp in range(NTB) for kc in range(4 * qp + 4)}
        cref = Tile(cref_t[:, :, :])

        class Ring:
            def __init__(self, tiles):
                self.tiles = tiles
                self.i = 0

            def next(self):
                t = self.tiles[self.i % len(self.tiles)]
                self.i += 1
                return t

        psA = Ring(ps[0:4])
        psB = Ring(ps[4:6])
        psC = Ring(ps[6:8])
        psO = Ring(ps[4:8])
        sgR = Ring(sgr); tmpR = Ring(tmp[4:7]); eR = Ring(ering); recR = Ring(recr); vsgR = Ring(vsgr); st6R = Ring(st6)
        wR = Ring(wslot)

        def mm(out, lhsT, rhs, start, stop):
            return lambda e: e.matmul(out, lhsT, rhs, start=start, stop=stop)

        def act(out, in_, func, reads, writes, bias=None, scale=None):
            kw = {}
            if bias is not None:
                kw["bias"] = bias
            if scale is not None:
                kw["scale"] = scale
            return P.op("act", lambda e: e.activation(out=out, in_=in_, func=func, **kw), reads, writes)

        def tt(out, in0, in1, op, reads, writes, eng="dve"):
            return P.op(eng, lambda e: e.tensor_tensor(out=out, in0=in0, in1=in1, op=op), reads, writes)

        def ts(out, in0, s1, s2, op0, op1, reads, writes):
            if op1 is None:
                return P.op("dve", lambda e: e.tensor_scalar(out=out, in0=in0, scalar1=s1, scalar2=None, op0=op0), reads, writes)
            return P.op("dve", lambda e: e.tensor_scalar(out=out, in0=in0, scalar1=s1, scalar2=s2, op0=op0, op1=op1), reads, writes)

        def stt(out, in0, scalar, in1, op0, op1, reads, writes):
            return P.op("dve", lambda e: e.scalar_tensor_tensor(out=out, in0=in0, scalar=scalar, in1=in1, op0=op0, op1=op1), reads, writes)

        def load_w(src_ap, F):
            t = wR.next()
            idx = wslot.index(t)
            dst = wr_t[:, idx, 0:F]
            if F % 1024 == 0:
                piece = 1024
            elif F == SLOT:
                piece = 704
            else:
                piece = 704
            for a in range(F // piece):
                dv = dst[:, a * piece:(a + 1) * piece]
                sv = src_ap[:, a * piece:(a + 1) * piece]
                P.dma("pool", lambda e, dv=dv, sv=sv: e.dma_start(out=dv, in_=sv), f"w{idx}", reads=(), writes=[t])
            return t, dst

        cnt_c = [0]

        def cload(tile_, dst_ap, src_ap):
            cnt_c[0] += 1
            P.dma("sp", lambda e: e.dma_start(out=dst_ap, in_=src_ap), f"c{cnt_c[0]}", reads=(), writes=[tile_])

        cload(lnp, lnp_t[:, :], c_lnp[:, :]); cload(convw, convw_t[:, :], c_convw[:, :])
        cload(sgb, sgb_t[:, :], c_sgb[:, :]); cload(bfb, bf_t[:, :], c_bf[:, :])
        wst32_ap = tmp_t[:, 0:2, :].rearrange('p a b -> p (a b)')
        for _i in range(2):
            cload(tmp[_i], tmp_t[:, _i, :], c_wst[:, _i * 512:(_i + 1) * 512])
        cload(bsb, bsb_t[:, :], c_bsb[:, :]); cload(tri, tri_t[:, :], c_tri[:, :])
        P.op("dve", lambda e: e.memset(ones32_t[:, :], 1.0), [], [ones32])
        P.op("dve", lambda e: e.memset(onesb_t[:, :], 1.0), [], [onesb])
        P.op("dve", lambda e: e.tensor_copy(out=trib_t[:, :], in_=tri_t[:, :]), [tri], [trib])
        for i in range(NL * 4):
            tt(wst_t[:, i * 128:(i + 1) * 128], wst32_ap[:, i * 128:(i + 1) * 128], tri_t[:, :], ALU.mult, [tmp[0], tmp[1], tri], [wst])

        def lnp_ap(l, i, c):
            o = (l * 6 + i) * KC + c
            return lnp_t[:, o:o + 1]

        def layer_norm(l, i, eps, after_tb=None):
            def stats_chunk(tb, c, S1, S2):
                act(xbf[c][tb].ap, x32[c][tb].ap, AF.Copy, [x32[c][tb]], [xbf[c][tb]])
                sq = sgR.next()
                act(sq.ap, x32[c][tb].ap, AF.Square, [x32[c][tb]], [sq])
                P.group("pe", [mm(S1.ap, onesb_t[:, :], xbf[c][tb].ap, c == 0, c == KC - 1)], [onesb, xbf[c][tb]], [S1])
                P.group("pe", [mm(S2.ap, onesb_t[:, :], sq.ap, c == 0, c == KC - 1)], [onesb, sq], [S2])

            def ab_chain(S1, S2):
                m2 = tmp[0]; var = tmp[1]; Asb = tmp[2]
                A = psB.next(); B = psB.next()
                act(m2.ap, S1.ap, AF.Square, [S1], [m2], scale=1.0 / D)
                stt(var.ap, S2.ap, 1.0 / D, m2.ap, ALU.mult, ALU.subtract, [S2, m2], [var])
                act(var.ap, var.ap, AF.Ln, [var], [var], bias=eps)
                act(Asb.ap, var.ap, AF.Exp, [var], [Asb], scale=-0.5)
                act(A.ap, Asb.ap, AF.Copy, [Asb], [A])
                stt(B.ap, S1.ap, -1.0 / D, Asb.ap, ALU.mult, ALU.mult, [S1, Asb], [B])
                return A, B

            def norm_group(tb, cs_, A, B):
                t1s = {c: tmpR.next() for c in cs_}
                for c in cs_:
                    tt(t1s[c].ap, x32[c][tb].ap, A.ap, ALU.mult, [x32[c][tb], A], [t1s[c]])
                for c in cs_:
                    tt(t1s[c].ap, t1s[c].ap, B.ap, ALU.add, [t1s[c], B], [t1s[c]])
                for c in cs_:
                    t1 = t1s[c]
                    ts(x32[c][tb].ap, t1.ap, lnp_ap(l, 2 * i, c), lnp_ap(l, 2 * i + 1, c), ALU.mult, ALU.add, [t1, lnp], [x32[c][tb]])
                    act(xbf[c][tb].ap, t1.ap, AF.Identity, [t1, lnp], [xbf[c][tb]], bias=lnp_ap(l, 2 * i + 1, c), scale=lnp_ap(l, 2 * i, c))

            Scur = (psC.next(), psC.next())
            for c in range(KC):
                stats_chunk(0, c, *Scur)
            AB = ab_chain(*Scur)
            for tb in range(NTB):
                nxt_ = tb + 1 < NTB
                if nxt_:
                    Scur = (psC.next(), psC.next())
                for c0 in range(0, KC, 3):
                    cs_ = list(range(c0, min(c0 + 3, KC)))
                    if nxt_:
                        for c in cs_:
                            stats_chunk(tb + 1, c, *Scur)
                    norm_group(tb, cs_, *AB)
                if nxt_:
                    AB = ab_chain(*Scur)
                if after_tb is not None:
                    after_tb(tb)

        def ffn_make(l, f):
            cres = 0.5 / ALPHA
            groups = [[0, 1, 2], [3, 4, 5], [6, 7, 8], [9, 10]]
            state = {}

            def up_tb(g, wt, tb):
                for jj in g:
                    wtile, wap = wt[jj]
                    w3 = wap.rearrange("p (k c) -> p k c", c=256)
                    G = psA.next(); U = psA.next()
                    rd = [wtile] + [xbf[kc][tb] for kc in range(KC)]
                    P.group("pe", [mm(G.ap, w3[:, kc, 0:128], xbf[kc][tb].ap, kc == 0, kc == KC - 1) for kc in range(KC)], rd, [G])
                    P.group("pe", [mm(U.ap, w3[:, kc, 128:256], xbf[kc][tb].ap, kc == 0, kc == KC - 1) for kc in range(KC)], rd, [U])
                    sg = sgR.next()
                    act(sg.ap, G.ap, AF.Silu, [G], [sg])
                    tt(hb_ap(jj, tb), sg.ap, U.ap, ALU.mult, [sg, U], hb_tl(jj, tb))

            def head(tb):
                if "wt" not in state:
                    state["wt"] = {jj: load_w(w_up[l, f, jj], 2048) for jj in groups[0]}
                up_tb(groups[0], state["wt"], tb)

            def rest():
                for hf in range(2):
                    for gi, g in enumerate(groups):
                        if hf == 0 and gi == 0:
                            continue
                        wt = {jj: load_w(w_up[l, f, hf * NJH + jj], 2048) for jj in g}
                        for tb in range(NTB):
                            up_tb(g, wt, tb)
                    for m in range(KC):
                        wtile, wap = load_w(w_dn[l, f, hf, m], NJH * 128)
                        w3 = wap.rearrange("p (j c) -> p j c", c=128)
                        for tb in range(NTB):
                            Y = psB.next()
                            rd = [wtile]
                            for jj in range(NJH):
                                rd += hb_tl(jj, tb)
                            P.group("pe", [mm(Y.ap, w3[:, jj, :], hb_ap(jj, tb), jj == 0, jj == NJH - 1) for jj in range(NJH)], rd, [Y])
                            stt(x32[m][tb].ap, Y.ap, cres, x32[m][tb].ap, ALU.mult, ALU.add, [Y, x32[m][tb]], [x32[m][tb]])

            return head, rest

        ROW_Q, ROW_K, ROW_U, ROW_YA = 0, 4, 8, 10

        def proj_fm(wtile, w3, c2, tb):
            Pp = psA.next()
            rd = [wtile] + [xbf[kc][tb] for kc in range(KC)]
            P.group("pe", [mm(Pp.ap, w3[:, kc, c2 * 128:(c2 + 1) * 128], xbf[kc][tb].ap, kc == 0, kc == KC - 1) for kc in range(KC)], rd, [Pp])
            return Pp

        def mixer_make(l):
            state = {}

            def conv_tb(tb):
                if "W" not in state:
                    WA = load_w(w_in[l, 0], 2048); WB = load_w(w_in[l, 1], 2048); WC = load_w(w_in[l, 2], 2048)
                    state["W"] = (WA, WB, WC)
                    for c in range(2):
                        P.op("dve", lambda e, c=c: e.memset(z_t[:, c, 0:2], 0.0), [], [zc[c]])
                WA, WB, WC = state["W"]
                WA3 = WA[1].rearrange("p (k c) -> p k c", c=256)
                WB3 = WB[1].rearrange("p (k c) -> p k c", c=256)
                WC3 = WC[1].rearrange("p (k c) -> p k c", c=256)
                for c in range(2):
                    def cw(i):
                        o = (l * 3 + i) * 2 + c
                        return convw_t[:, o:o + 1]
                    P1 = proj_fm(WA[0], WA3, c, tb)
                    cct = tmpR.next()
                    act(cct.ap, P1.ap, AF.Copy, [P1], [cct])
                    P2 = proj_fm(WB[0], WB3, c, tb)
                    tt(z_t[:, c, 2:514], P2.ap, cct.ap, ALU.mult, [P2, cct], [zc[c]])
                    yv = tmpR.next()
                    ts(yv.ap, z_t[:, c, 2:514], cw(2), None, ALU.mult, None, [zc[c], convw], [yv])
                    stt(yv.ap, z_t[:, c, 1:513], cw(1), yv.ap, ALU.mult, ALU.add, [zc[c], convw, yv], [yv])
                    stt(yv.ap, z_t[:, c, 0:512], cw(0), yv.ap, ALU.mult, ALU.add, [zc[c], convw, yv], [yv])
                    P3 = proj_fm(WC[0], WC3, c, tb)
                    tt(hb_ap(ROW_YA + c, tb), P3.ap, yv.ap, ALU.mult, [P3, yv], hb_tl(ROW_YA + c, tb))
                    P.op("dve", lambda e, c=c: e.tensor_copy(out=z_t[:, c, 0:2], in_=z_t[:, c, 512:514]), [zc[c]], [zc[c]])

            def rest():
                mixer_rest(l)

            return conv_tb, rest

        def mixer_rest(l):
            if STAGE == 'm1a':
                return
            for ti, row0, kind in ((3, ROW_Q, "c"), (4, ROW_Q + 2, "c"), (5, ROW_K, "c"), (6, ROW_K + 2, "c"), (7, ROW_U, "g")):
                W = load_w(w_in[l, ti], 2048)
                W3 = W[1].rearrange("p (k c) -> p k c", c=256)
                for tb in range(NTB):
                    for c2 in range(2):
                        Pp = proj_fm(W[0], W3, c2, tb)
                        fn = AF.Copy if kind == "c" else AF.Gelu_apprx_tanh
                        act(hb_ap(row0 + c2, tb), Pp.ap, fn, [Pp], hb_tl(row0 + c2, tb))
            if STAGE == 'm1b':
                return
            WF = load_w(w_fsv[l], SLOT); WV0 = load_w(w_v[l, 0], 2048); WV1 = load_w(w_v[l, 1], 2048)
            WF3 = WF[1].rearrange("p (k c) -> p k c", c=264)
            WV3 = [WV0[1].rearrange("p (k c) -> p k c", c=512), WV1[1].rearrange("p (k c) -> p k c", c=512)]
            GN = 4
            grp_s6 = {}

            def phaseA(gi):
                s6g = st6R.next()
                grp_s6[gi] = s6g
                for j in range(GN):
                    n = gi * GN + j
                    tb = n // 4
                    cs = slice((n % 4) * 128, (n % 4) * 128 + 128)
                    xs = [xbfn[kc][n] for kc in range(KC)]
                    T1 = psA.next()
                    P.group("pe", [mm(T1.ap[:, 0:264], xbf[kc][tb].ap[:, cs], WF3[:, kc, :], kc == 0, kc == KC - 1) for kc in range(KC)], [WF[0]] + xs, [T1])
                    T2 = psA.next()
                    fns = [mm(T2.ap[:, 0:512], xbf[kc][tb].ap[:, cs], WV3[kc // 4][:, kc % 4, :], kc == 0, kc == KC - 1) for kc in range(KC)]
                    P.group("pe", fns, [WV0[0], WV1[0]] + xs, [T2])
                    tt(zf_t[:, n * NH:(n + 1) * NH], T1.ap[:, 256:264], bf_t[:, l * NH:(l + 1) * NH], ALU.add, [T1, bfb], [zf])
                    gv = gvt[(gi % 2) * GN + j]
                    P.op("act", lambda e, gv=gv, T1=T1, j=j: e.activation(out=gv.ap, in_=T1.ap[:, 0:256], func=AF.Gelu_apprx_tanh, accum_out=s6g.ap[:, 0, j:j + 1]), [T1], [gv, s6g])
                    P.op("act", lambda e, gv=gv, j=j: e.activation(out=z_t[:, 0, 0:256], in_=gv.ap, func=AF.Square, accum_out=s6g.ap[:, 1, j:j + 1]), [gv], [zc[0], s6g])
                    T2v = T2.ap.rearrange("p (h d) -> p h d", d=64)
                    c0 = n * 128
                    P.op("act", lambda e, T2v=T2v, c0=c0: e.activation(out=xbf_t[:, 0:8:2, c0:c0 + 64], in_=T2v[:, 0:8:2, :], func=AF.Copy), [T2] + xs, xs)
                    P.op("act", lambda e, T2v=T2v, c0=c0: e.activation(out=xbf_t[:, 1:8:2, c0 + 64:c0 + 128], in_=T2v[:, 1:8:2, :], func=AF.Copy), [T2] + xs, xs)
                    P.op("dve", lambda e, c0=c0: e.memset(xbf_t[:, 0:8:2, c0 + 64:c0 + 128], 1.0), xs, xs)
                    P.op("dve", lambda e, c0=c0: e.memset(xbf_t[:, 1:8:2, c0:c0 + 64], 1.0), xs, xs)

            def phaseB(gi):
                s6g = grp_s6[gi]
                f = lambda k: s6g.ap[:, k, :]
                ts(f(2), f(0), 1.0 / 256, None, ALU.mult, None, [s6g], [s6g])
                tt(f(3), f(2), f(2), ALU.mult, [s6g], [s6g])
                stt(f(4), f(1), 1.0 / 256, f(3), ALU.mult, ALU.subtract, [s6g], [s6g])
                act(f(4), f(4), AF.Ln, [s6g], [s6g], bias=LN_EPS)
                act(f(5), f(4), AF.Exp, [s6g], [s6g], scale=-0.5)
                stt(f(6), f(2), -1.0, f(5), ALU.mult, ALU.mult, [s6g], [s6g])

            def phaseC(gi):
                s6g = grp_s6.pop(gi)
                for j in range(GN):
                    n = gi * GN + j
                    gv = gvt[(gi % 2) * GN + j]
                    act(gv.ap, gv.ap, AF.Identity, [gv, s6g], [gv], bias=s6g.ap[:, 6, j:j + 1], scale=s6g.ap[:, 5, j:j + 1])
                    tt(gv.ap, gv.ap, sgb_t[:, (l * 2) * 256:(l * 2 + 1) * 256], ALU.mult, [gv, sgb], [gv])
                    vs = vsgR.next()
                    tt(vs.ap, gv.ap, sgb_t[:, (l * 2 + 1) * 256:(l * 2 + 2) * 256], ALU.add, [gv, sgb], [vs])
                    for cp in range(2):
                        Mx = psC.next()
                        fns = []
                        for gg in range(2):
                            g4 = 2 * cp + gg
                            o = (l * 4 + g4) * 128
                            fns.append(mm(Mx.ap[gg * 64:(gg + 1) * 64, 0:128], vs.ap[:, g4 * 64:(g4 + 1) * 64], wst_t[:, o:o + 128], True, True))
                        P.group("pe", fns, [vs, wst], [Mx])
                        t1 = tmpR.next()
                        ob = (l * 2 + cp) * 128
                        tt(t1.ap[:, 0:128], Mx.ap[:, 0:128], bsb_t[:, ob:ob + 128], ALU.add, [Mx, bsb], [t1])
                        utl = hb[ROW_U + cp][n // 2]
                        uap = hb_t[:, ROW_U + cp, n * 128:(n + 1) * 128]
                        tt(uap, t1.ap[:, 0:128], uap, ALU.mult, [t1, utl], [utl])

            def m2_chain():
                NF = NCH * NH
                mn, ab, ex, l1 = lf
                ts(mn.ap, zf_t[:, :], 0.0, None, ALU.min, None, [zf], [mn])
                stt(ab.ap, mn.ap, 2.0, zf_t[:, :], ALU.mult, ALU.subtract, [mn, zf], [ab])
                act(ex.ap, ab.ap, AF.Exp, [ab], [ex])
                act(l1.ap, ex.ap, AF.Ln, [ex], [l1], bias=1.0)
                tt(mn.ap, mn.ap, l1.ap, ALU.subtract, [mn, l1], [mn])
                CW = psB.next(); TOT = psB.next()
                P.group("pe", [mm(CW.ap[:, 0:NF], tri_t[:, :], mn.ap, True, True)], [tri, mn], [CW])
                P.group("pe", [mm(TOT.ap[:, 0:NF], ones32_t[:, :], mn.ap, True, True)], [ones32, mn], [TOT])
                P.op("dve", lambda e: e.memset(offs_t[:, 0:NH], 0.0), [], [offs])
                for n in range(1, NCH + 1):
                    tt(offs_t[:, n * NH:(n + 1) * NH], offs_t[:, (n - 1) * NH:n * NH], TOT.ap[:, (n - 1) * NH:n * NH], ALU.add, [offs, TOT], [offs])
                tt(cum_t[:, :], CW.ap[:, 0:NF], offs_t[:, 0:NF], ALU.add, [CW, offs], [cum])
                for qp in range(NTB):
                    o0 = 4 * qp * NH
                    o1 = (4 * qp + 4) * NH
                    dq = cref_t[:, 0, qp * NH:(qp + 1) * NH]
                    cr = cref_t[:, 1, qp * NH:(qp + 1) * NH]
                    tt(dq, offs_t[:, o0:o0 + NH], offs_t[:, o1:o1 + NH], ALU.subtract, [offs], [cref])
                    ts(dq, dq, -60.0, 0.0, ALU.add, ALU.max, [cref], [cref])
                    tt(cr, offs_t[:, o0:o0 + NH], dq, ALU.subtract, [offs, cref], [cref])
                for qp in range(NTB):
                    cr = cref_t[:, 1, qp * NH:(qp + 1) * NH]
                    for kc in range(4 * qp + 4):
                        o = (2 * qp * (qp + 1) + kc) * NH
                        tt(bias_t[:, o:o + NH], cr, cum_t[:, kc * NH:(kc + 1) * NH], ALU.subtract, [cref, cum], [biasT[(qp, kc)]])

            NG = NCH // GN
            for gi in range(NG + 1):
                if gi < NG:
                    phaseA(gi)
                if gi == NG - 1 and STAGE not in ('m1c',):
                    m2_chain()
                if gi >= 1:
                    phaseC(gi - 1)
                if gi < NG:
                    phaseB(gi)
            if STAGE == 'm1c':
                return
            if STAGE == 'm2':
                return
            NQP = NQB // 2
            steps = []
            for c in range(NH // 2):
                for qp in range(NQP):
                    for kc in range(4 * qp + 4):
                        steps.append((c, qp, kc))
            Ocur = {}

            def emit_score(c, qp, kc):
                qts = [hb[ROW_Q + c][2 * qp], hb[ROW_Q + c][2 * qp + 1]]
                kt = hb[ROW_K + c][kc // 2]
                st0 = max(0, kc - 4 * qp) * 128
                STs = [psA.next(), psA.next()]
                Es = [eR.next(), eR.next()]
                fns = []
                for hp in range(2):
                    ps_ = slice(hp * 64, hp * 64 + 64)
                    kap = hb_t[ps_, ROW_K + c, kc * 128:(kc + 1) * 128]
                    qap = hb_t[ps_, ROW_Q + c, qp * 512 + st0:(qp + 1) * 512]
                    fns.append(mm(STs[hp].ap[:, st0:512], kap, qap, True, True))
                P.group("pe", fns, [kt] + qts, STs)
                for hp in range(2):
                    h = 2 * c + hp
                    ST = STs[hp]; E = Es[hp]
                    o = (2 * qp * (qp + 1) + kc) * NH + h
                    act(E.ap[:, st0:512], ST.ap[:, st0:512], AF.Exp, [ST, biasT[(qp, kc)]], [E], bias=bias_t[:, o:o + 1], scale=0.125)
                    if kc >= 4 * qp:
                        tt(E.ap[:, st0:st0 + 128], E.ap[:, st0:st0 + 128], trib_t[:, :], ALU.mult, [E, trib], [E])
                return Es

            def emit_pv(c, qp, kc, Es):
                nk = 4 * qp + 4
                st0 = max(0, kc - 4 * qp) * 128
                if kc == 0:
                    Ocur[(c, qp)] = [psO.next(), psO.next()]
                Os = Ocur[(c, qp)]
                fns = []
                rd = []
                for hp in range(2):
                    h = 2 * c + hp
                    rd += [xbfn[h][kc], Es[hp]]
                    vap = xbf_t[:, h, kc * 128:(kc + 1) * 128]
                    fns.append(mm(Os[hp].ap[:, st0:512], vap, Es[hp].ap[:, st0:512], kc == 0, kc == nk - 1))
                P.group("pe", fns, rd, Os)
                if kc == nk - 1:
                    qts = [hb[ROW_Q + c][2 * qp], hb[ROW_Q + c][2 * qp + 1]]
                    rc = recR.next()
                    for hp in range(2):
                        O = Os[hp]
                        ps_ = slice(hp * 64, hp * 64 + 64)
                        dn = slice(64, 128) if hp == 0 else slice(0, 64)
                        P.op("dve", lambda e, rc=rc, O=O, dn=dn, ps_=ps_: e.reciprocal(out=rc.ap[ps_, :], in_=O.ap[dn, :]), [O], [rc])
                        yap = hb_t[ps_, ROW_Q + c, qp * 512:(qp + 1) * 512]
                        tt(yap, O.ap[ps_, :], rc.ap[ps_, :], ALU.mult, [O, rc], qts)

            pend = []
            for st_ in steps:
                if len(pend) == 2:
                    st0_, E0 = pend.pop(0)
                    emit_pv(*st0_, E0)
                Es_ = emit_score(*st_)
                pend.append((st_, Es_))
            for st0_, E0 in pend:
                emit_pv(*st0_, E0)
            if STAGE == 'm3':
                return
            yrows = [ROW_YA, ROW_YA + 1, ROW_Q, ROW_Q + 1, ROW_Q + 2, ROW_Q + 3, ROW_U, ROW_U + 1]
            for mp in range(4):
                W = load_w(w_o[l, mp], 2048)
                W3 = W[1].rearrange("p (k c) -> p k c", c=256)
                for tb in range(NTB):
                    for m2 in range(2):
                        m = mp * 2 + m2
                        Y = psB.next()
                        rd = [W[0]]
                        for r in yrows:
                            rd += hb_tl(r, tb)
                        P.group("pe", [mm(Y.ap, W3[:, kc, m2 * 128:(m2 + 1) * 128], hb_ap(yrows[kc], tb), kc == 0, kc == KC - 1) for kc in range(KC)], rd, [Y])
                        stt(x32[m][tb].ap, Y.ap, 1.0 / ALPHA, x32[m][tb].ap, ALU.mult, ALU.add, [Y, x32[m][tb]], [x32[m][tb]])

        eps_s = LN_EPS / (ALPHA * ALPHA)
        def load_blk(s, tb):
            for c in range(KC):
                P.dma("sp", lambda e, c=c: e.dma_start(out=x32[c][tb].ap, in_=xT[s, c * 128:(c + 1) * 128, tb * 512:(tb + 1) * 512]), f"xi{c}_{tb}", reads=(), writes=[x32[c][tb]])
            for c in range(KC):
                act(xbf[c][tb].ap, x32[c][tb].ap, AF.Copy, [x32[c][tb]], [xbf[c][tb]])

        def store_blk(s, tb):
            for c in range(KC):
                P.dma("sp", lambda e, c=c: e.dma_start(out=outT[s, c * 128:(c + 1) * 128, tb * 512:(tb + 1) * 512], in_=x32[c][tb].ap), f"xo{c}_{tb}", reads=[x32[c][tb]], writes=())

        pre = None
        for s in range(NSEQ):
            if pre is None:
                if STAGE == "full":
                    nxt = ffn_make(LAYERS[0], 0)
                    for tb in range(NTB):
                        load_blk(s, tb)
                        nxt[0](tb)
                else:
                    for tb in range(NTB):
                        load_blk(s, tb)
                    nxt = None
            else:
                nxt = pre
                pre = None
            for li, l in enumerate(LAYERS):
                if nxt is None:
                    f1_head, f1_rest = ffn_make(l, 0)
                    for tb in range(NTB):
                        f1_head(tb)
                else:
                    f1_head, f1_rest = nxt
                f1_rest()
                if STAGE == "ffn1":
                    layer_norm(l, 0, eps_s)
                    nxt = None
                    continue
                conv_tb, mix_rest = mixer_make(l)
                layer_norm(l, 0, eps_s, after_tb=conv_tb)
                mix_rest()
                if STAGE.startswith("m") and STAGE != "mix":
                    nxt = None
                    continue
                if STAGE == "mix":
                    layer_norm(l, 1, eps_s)
                    nxt = None
                    continue
                f2_head, f2_rest = ffn_make(l, 1)
                layer_norm(l, 1, eps_s, after_tb=f2_head)
                f2_rest()
                if li + 1 < len(LAYERS):
                    nxt = ffn_make(LAYERS[li + 1], 0)
                    layer_norm(l, 2, eps_s, after_tb=nxt[0])
                else:
                    nxt = None
                    if s + 1 < NSEQ:
                        pre = ffn_make(LAYERS[0], 0)

                    def io_hook(tb, s=s, pre=pre):
                        store_blk(s, tb)
                        if pre is not None:
                            load_blk(s + 1, tb)
                            pre[0](tb)

                    layer_norm(l, 2, eps_s, after_tb=io_hook)
            if STAGE != "full":
                for tb in range(NTB):
                    store_blk(s, tb)
        P.final_wait("sp", [f"xo{c}_{tb}" for c in range(KC) for tb in range(NTB)])

        for k in list(P.cnt.keys()):
            sem(k)

        def replay(engname):
            def body(e):
                for it in P.streams[engname]:
                    if it[0] == "wait":
                        e.wait_ge(sems[it[1]], it[2])
                    else:
                        ins = it[1](e)
                        if it[3]:
                            ins.then_inc(sems[it[2]], it[3])
            return body

        with nc.Block() as block:
            block.tensor(replay("pe"))
            block.scalar(replay("act"))
            block.vector(replay("dve"))
            block.gpsimd(replay("pool"))
            block.sync(replay("sp"))
    return nc


def _prep_weights(inp, NL=2):
    f32 = np.float32
    def up_tiles(w):
        a = w.reshape(NL, KC, 128, 2, NJ, 128)
        a = a.transpose(0, 4, 2, 1, 3, 5)
        return np.ascontiguousarray(a.reshape(NL, NJ, 128, 2048))
    def dn_tiles(w):
        a = w.reshape(NL, 2, NJH, 128, KC, 128)
        a = a.transpose(0, 1, 4, 3, 2, 5)
        return np.ascontiguousarray(a.reshape(NL, 2, KC, 128, NJH * 128))
    def col_tile(w, cols):
        a = w[:, :, cols].reshape(NL, KC, 128, len(cols))
        a = a.transpose(0, 2, 1, 3)
        return np.ascontiguousarray(a.reshape(NL, 128, KC * len(cols)))
    w_up = np.stack([up_tiles(inp["ffn1_w_up"]), up_tiles(inp["ffn2_w_up"])], axis=1).astype(f32)
    w_dn = np.stack([dn_tiles(inp["ffn1_w_down"]), dn_tiles(inp["ffn2_w_down"])], axis=1).astype(f32)
    wi = inp["mix_w_in"]
    r = np.arange
    tiles = [r(256, 512), r(512, 768), r(0, 256), r(768, 1024), r(1024, 1280), r(1280, 1536), r(1536, 1792), r(2312, 2568)]
    w_in = np.stack([col_tile(wi, t) for t in tiles], axis=1).astype(f32)
    w_fsv = col_tile(wi, np.concatenate([r(2568, 2824), r(2304, 2312)])).astype(f32)
    wv = wi[:, :, 1792:2304].reshape(NL, 2, 4, 128, 512).transpose(0, 1, 3, 2, 4)
    w_v = np.ascontiguousarray(wv.reshape(NL, 2, 128, 2048)).astype(f32)
    wo = inp["mix_w_out"]
    w_o = np.stack([col_tile(wo, r(mp * 256, (mp + 1) * 256)) for mp in range(4)], axis=1).astype(f32)
    lnl = [inp["ln1_g"], inp["ln1_b"], inp["ln2_g"], inp["ln2_b"], inp["ln3_g"], inp["ln3_b"]]
    lnp = np.stack(lnl, axis=1).reshape(NL, 6, KC, 128).transpose(3, 0, 1, 2).reshape(128, NL * 6 * KC)
    convw = inp["conv_w"].reshape(NL, 3, 2, 128).transpose(3, 0, 1, 2).reshape(128, NL * 3 * 2)
    sgb = np.stack([inp["sgu_ln_g"], inp["sgu_ln_b"]], axis=1).reshape(1, NL * 2 * 256)
    sgb = np.broadcast_to(sgb, (128, NL * 2 * 256))
    bfb = np.broadcast_to(inp["fox_b_f"].reshape(1, NL * NH), (128, NL * NH))
    wst = inp["sgu_w_s"].transpose(3, 0, 1, 2).reshape(128, NL * 4 * 128)
    bs = inp["sgu_b_s"].reshape(NL, 2, 2, 1, 128)
    bsb = np.broadcast_to(bs, (NL, 2, 2, 64, 128)).transpose(2, 3, 0, 1, 4).reshape(128, NL * 2 * 128)
    tri = (np.arange(128)[None, :] >= np.arange(128)[:, None]).astype(f32)
    c = lambda a: np.ascontiguousarray(a, dtype=f32)
    return dict(w_up=w_up, w_dn=w_dn, w_in=w_in, w_fsv=w_fsv, w_v=w_v, w_o=w_o, c_lnp=c(lnp), c_convw=c(convw),
                c_sgb=c(sgb), c_bf=c(bfb), c_wst=c(wst), c_bsb=c(bsb), c_tri=c(tri))


def kernel(**inputs):
    inp = {k: np.asarray(v) for k, v in inputs.items()}
    x = inp["x"]
    B, S, _ = x.shape
    ncores = 8
    per = B // ncores
    wd = _prep_weights(inp)
    nc = build_nc(NSEQ=per, NTB=S // 512)
    in_maps = []
    for i in range(ncores):
        m = dict(wd)
        m["xT"] = np.ascontiguousarray(x[i * per:(i + 1) * per].transpose(0, 2, 1))
        in_maps.append(m)
    res = run_bass_kernel_spmd(nc, in_maps, core_ids=list(range(ncores)))
    outs = [np.asarray(r["outT"]).transpose(0, 2, 1) for r in res.results]
    return np.ascontiguousarray(np.concatenate(outs, axis=0)).astype(np.float32)
```
